# Optimizing a Trainium2 kernel written in Bass

```python
import jax
import jax.numpy as jnp
from jax import lax
import numpy as np

D_MODEL = 1024
BATCH = 8
SEQ = 2048
DEPTH = 2

N_MEM = 256
EXPAND = 2
D_INNER = EXPAND * D_MODEL
D_XATTN = D_INNER // 4
XATTN_HEADS = 4
XATTN_HEAD_DIM = D_XATTN // XATTN_HEADS
D_MIX = D_INNER - D_XATTN

MLSTM_HEADS = 4
MLSTM_HEAD_DIM = D_MIX // MLSTM_HEADS
MLSTM_CONV = 4
QKV_BLOCK = 4
N_QKV_BLOCKS = D_MIX // QKV_BLOCK
MLSTM_CHUNK = 64
ML_IN_W = D_MIX + D_XATTN + D_INNER

RWKV_HEAD_DIM = 64
RWKV_HEADS = D_MIX // RWKV_HEAD_DIM
DECAY_RANK = 64
ICLR_RANK = 64
VRES_RANK = 32
GATE_RANK = 128
RW_SHIFT_W = 3 * D_MIX + DECAY_RANK + ICLR_RANK + VRES_RANK + GATE_RANK
RW_IN_W = RW_SHIFT_W + D_XATTN + D_INNER

N_MLSTM = (DEPTH + 1) // 2
N_RWKV = DEPTH // 2

RMS_EPS = 1e-6
MHLN_EPS = 1e-5
RWKV_GN_EPS = 64e-5
L2_EPS = 1e-12

kernel_name = 'hybrid_mlstm_rwkv7_memxattn'


def rms_norm(x, g):
    xf = x.astype(jnp.float32)
    y = xf * lax.rsqrt(jnp.mean(xf * xf, axis=-1, keepdims=True) + RMS_EPS)
    return (y * g.astype(jnp.float32)).astype(x.dtype)


def head_norm(x, n_heads, eps):
    xf = x.astype(jnp.float32).reshape(x.shape[:-1] + (n_heads, -1))
    mu = jnp.mean(xf, axis=-1, keepdims=True)
    var = jnp.mean(jnp.square(xf - mu), axis=-1, keepdims=True)
    return ((xf - mu) * lax.rsqrt(var + eps)).reshape(x.shape)


def causal_dwconv(x, w, b):
    k_w, c = w.shape
    y = lax.conv_general_dilated(x, w[:, None, :].astype(x.dtype), window_strides=(1,),
                                 padding=[(k_w - 1, 0)], dimension_numbers=('NWC', 'WIO', 'NWC'),
                                 feature_group_count=c)
    return y + b.astype(x.dtype)


def blockdiag_linear(x, w):
    xb = x.reshape(x.shape[:-1] + w.shape[:2])
    return jnp.einsum('bsni,nio->bsno', xb, w.astype(x.dtype)).reshape(x.shape)


def token_shift_mix(p, mu):
    prev = jnp.pad(p, ((0, 0), (1, 0), (0, 0)))[:, :-1]
    return p + (prev - p) * mu.astype(p.dtype)


def mlstm_chunkwise(q, k, v, i_pre, logf):
    bsz, nh, seq, dh = q.shape
    n_chunks = seq // MLSTM_CHUNK

    def chunks(t):
        t = t.reshape(t.shape[:2] + (n_chunks, MLSTM_CHUNK) + t.shape[3:])
        return jnp.moveaxis(t, 2, 0)

    causal = jnp.tril(jnp.ones((MLSTM_CHUNK, MLSTM_CHUNK), dtype=bool))

    def step(carry, inp):
        c_st, n_st, m_st = carry
        qb, kb, vb, ib, fb = inp
        b = jnp.cumsum(fb, axis=-1)
        d_mat = jnp.where(causal, b[..., :, None] - b[..., None, :] + ib[..., None, :], -jnp.inf)
        inter = b + m_st[..., None]
        m_t = jnp.maximum(jnp.max(d_mat, axis=-1), inter)
        s = jnp.einsum('bhtd,bhsd->bhts', qb, kb) * jnp.exp(d_mat - m_t[..., None])
        g_in = jnp.exp(inter - m_t)
        num = jnp.einsum('bhts,bhse->bhte', s, vb) + g_in[..., None] * jnp.einsum('bhtd,bhde->bhte', qb, c_st)
        den = jnp.sum(s, axis=-1) + g_in * jnp.einsum('bhtd,bhd->bht', qb, n_st)
        h = num / jnp.maximum(jnp.abs(den), jnp.exp(-m_t))[..., None]
        b_last = b[..., -1]
        w_s = b_last[..., None] - b + ib
        m_new = jnp.maximum(b_last + m_st, jnp.max(w_s, axis=-1))
        kw = kb * jnp.exp(w_s - m_new[..., None])[..., None]
        g_old = jnp.exp(b_last + m_st - m_new)
        c_new = g_old[..., None, None] * c_st + jnp.einsum('bhsd,bhse->bhde', kw, vb)
        n_new = g_old[..., None] * n_st + jnp.sum(kw, axis=2)
        return (c_new, n_new, m_new), h

    init = (jnp.zeros((bsz, nh, dh, dh), jnp.float32), jnp.zeros((bsz, nh, dh), jnp.float32),
            jnp.zeros((bsz, nh), jnp.float32))
    _, hs = lax.scan(step, init, (chunks(q), chunks(k), chunks(v), chunks(i_pre), chunks(logf)))
    return jnp.moveaxis(hs, 0, 2).reshape(bsz, nh, seq, dh)


def mlstm_mixer(u, conv_w, conv_b, wq, wk, wv, w_gate, b_gate, mhn_g, skip):
    bsz, seq, _ = u.shape
    xc = jax.nn.silu(causal_dwconv(u, conv_w, conv_b))
    q = blockdiag_linear(xc, wq)
    k = blockdiag_linear(xc, wk)
    v = blockdiag_linear(u, wv)
    gates = (jnp.concatenate([q, k, v], axis=-1) @ w_gate.astype(u.dtype)).astype(jnp.float32) + b_gate
    i_pre = jnp.moveaxis(gates[..., :MLSTM_HEADS], -1, 1)
    logf = jax.nn.log_sigmoid(jnp.moveaxis(gates[..., MLSTM_HEADS:], -1, 1))

    def heads(t):
        return t.astype(jnp.float32).reshape(bsz, seq, MLSTM_HEADS, MLSTM_HEAD_DIM).transpose(0, 2, 1, 3)

    h = mlstm_chunkwise(heads(q), heads(k) * (MLSTM_HEAD_DIM ** -0.5), heads(v), i_pre, logf)
    h = h.transpose(0, 2, 1, 3).reshape(bsz, seq, D_MIX)
    h = head_norm(h, MLSTM_HEADS, MHLN_EPS) * mhn_g + skip * xc.astype(jnp.float32)
    return h, v


def wkv7_scan(r, w, k, v, a, b):
    bsz, _, nh, n = r.shape

    def step(state, inp):
        r_t, w_t, k_t, v_t, a_t, b_t = inp
        sa = jnp.einsum('bhij,bhj->bhi', state, a_t)
        state = (state * w_t[..., None, :] + sa[..., :, None] * b_t[..., None, :]
                 + v_t[..., :, None] * k_t[..., None, :])
        return state, jnp.einsum('bhij,bhj->bhi', state, r_t)

    s0 = jnp.zeros((bsz, nh, n, n), jnp.float32)
    xs = tuple(jnp.moveaxis(t, 1, 0) for t in (r, w, k, v, a, b))
    _, out = lax.scan(step, s0, xs)
    return jnp.moveaxis(out, 0, 1)


def rwkv7_mixer(p, v_first, w_lora2, w0, a_lora2, a0, v_lora2, v0, g_lora2, k_k, k_a, r_k, lnx_g, lnx_b):
    p = p.astype(jnp.float32)
    bsz, seq, _ = p.shape
    cuts = [D_MIX, 2 * D_MIX, 3 * D_MIX, 3 * D_MIX + DECAY_RANK, 3 * D_MIX + DECAY_RANK + ICLR_RANK,
            3 * D_MIX + DECAY_RANK + ICLR_RANK + VRES_RANK]
    r, k, v, wl, al, vl, gl = jnp.split(p, cuts, axis=-1)
    logw = -jax.nn.softplus(-(w0 + jnp.tanh(wl) @ w_lora2)) - 0.5
    decay = jnp.exp(-jnp.exp(logw))
    a = jax.nn.sigmoid(a0 + al @ a_lora2)
    v = v + (v_first.astype(jnp.float32) - v) * jax.nn.sigmoid(v0 + vl @ v_lora2)
    g = jax.nn.sigmoid(gl) @ g_lora2

    def heads(t):
        return t.reshape(bsz, seq, RWKV_HEADS, RWKV_HEAD_DIM)

    kk = heads(k * k_k)
    kk = kk / jnp.maximum(jnp.sqrt(jnp.sum(kk * kk, axis=-1, keepdims=True)), L2_EPS)
    k = k * (1.0 + (a - 1.0) * k_a)
    rh, kh, vh, ah = heads(r), heads(k), heads(v), heads(a)
    out = wkv7_scan(rh, heads(decay), kh, vh, -kk, kk * ah)
    out = head_norm(out.reshape(bsz, seq, D_MIX), RWKV_HEADS, RWKV_GN_EPS) * lnx_g + lnx_b
    bonus = jnp.sum(rh * kh * r_k, axis=-1, keepdims=True) * vh
    return (out + bonus.reshape(bsz, seq, D_MIX)) * g


def memory_attention(qm, mem_n, w_kv):
    bsz, seq, _ = qm.shape
    km, vm = jnp.split(mem_n @ w_kv.astype(mem_n.dtype), 2, axis=-1)
    q = qm.reshape(bsz, seq, XATTN_HEADS, XATTN_HEAD_DIM)
    km = km.reshape(bsz, -1, XATTN_HEADS, XATTN_HEAD_DIM)
    vm = vm.reshape(bsz, -1, XATTN_HEADS, XATTN_HEAD_DIM)
    s = jnp.einsum('bshd,bmhd->bhsm', q, km).astype(jnp.float32) * (XATTN_HEAD_DIM ** -0.5)
    pr = jax.nn.softmax(s, axis=-1)
    o = jnp.einsum('bhsm,bmhd->bshd', pr, vm.astype(jnp.float32))
    return o.reshape(bsz, seq, D_XATTN)


def setup_inputs(seed: int = 0) -> dict:
    key = jax.random.key(seed)
    ks = iter(jax.random.split(key, 48))
    f32 = jnp.float32

    def nrm(shape, scale):
        return jax.random.normal(next(ks), shape, f32) * scale

    def gain(shape):
        return 1.0 + nrm(shape, 0.02)

    x = nrm((BATCH, SEQ, D_MODEL), 1.0)
    mem = nrm((BATCH, N_MEM, D_MODEL), 1.0)
    norm_g = gain((DEPTH, D_MODEL))
    mem_norm_g = gain((DEPTH, D_MODEL))
    mem_kv_w = nrm((DEPTH, D_MODEL, 2 * D_XATTN), D_MODEL ** -0.5)
    w_out = nrm((DEPTH, D_INNER, D_MODEL), D_INNER ** -0.5)
    ml_w_in = nrm((N_MLSTM, D_MODEL, ML_IN_W), D_MODEL ** -0.5)
    ml_conv_w = nrm((N_MLSTM, MLSTM_CONV, D_MIX), MLSTM_CONV ** -0.5)
    ml_conv_b = nrm((N_MLSTM, D_MIX), 0.01)
    ml_wq = nrm((N_MLSTM, N_QKV_BLOCKS, QKV_BLOCK, QKV_BLOCK), QKV_BLOCK ** -0.5)
    ml_wk = nrm((N_MLSTM, N_QKV_BLOCKS, QKV_BLOCK, QKV_BLOCK), QKV_BLOCK ** -0.5)
    ml_wv = nrm((N_MLSTM, N_QKV_BLOCKS, QKV_BLOCK, QKV_BLOCK), QKV_BLOCK ** -0.5)
    ml_w_gate = nrm((N_MLSTM, 3 * D_MIX, 2 * MLSTM_HEADS), (3 * D_MIX) ** -0.5)
    fgate_bias = jnp.linspace(3.0, 6.0, MLSTM_HEADS, dtype=f32)
    ml_b_gate = jnp.concatenate([nrm((N_MLSTM, MLSTM_HEADS), 0.1),
                                 fgate_bias[None] + nrm((N_MLSTM, MLSTM_HEADS), 0.1)], axis=-1)
    ml_mhn_g = gain((N_MLSTM, D_MIX))
    ml_skip = gain((N_MLSTM, D_MIX))
    rw_w_in = nrm((N_RWKV, D_MODEL, RW_IN_W), D_MODEL ** -0.5)
    rw_mu = jax.random.uniform(next(ks), (N_RWKV, RW_SHIFT_W), f32, 0.0, 1.0)
    rw_w_lora2 = nrm((N_RWKV, DECAY_RANK, D_MIX), 0.5 * DECAY_RANK ** -0.5)
    chan = jnp.linspace(0.0, 1.0, D_MIX, dtype=f32)
    rw_w0 = (-6.5 + 5.0 * chan ** 0.85)[None] + nrm((N_RWKV, D_MIX), 0.1)
    rw_a_lora2 = nrm((N_RWKV, ICLR_RANK, D_MIX), 0.5 * ICLR_RANK ** -0.5)
    rw_a0 = nrm((N_RWKV, D_MIX), 0.1)
    rw_v_lora2 = nrm((N_RWKV, VRES_RANK, D_MIX), 0.5 * VRES_RANK ** -0.5)
    rw_v0 = 1.0 + nrm((N_RWKV, D_MIX), 0.1)
    rw_g_lora2 = nrm((N_RWKV, GATE_RANK, D_MIX), GATE_RANK ** -0.5)
    rw_k_k = 0.85 + nrm((N_RWKV, D_MIX), 0.02)
    rw_k_a = 1.0 + nrm((N_RWKV, D_MIX), 0.02)
    rw_r_k = -0.04 + nrm((N_RWKV, RWKV_HEADS, RWKV_HEAD_DIM), 0.02)
    rw_lnx_g = gain((N_RWKV, D_MIX))
    rw_lnx_b = nrm((N_RWKV, D_MIX), 0.01)
    final_g = gain((D_MODEL,))
    return {'x': x, 'mem': mem, 'norm_g': norm_g, 'mem_norm_g': mem_norm_g, 'mem_kv_w': mem_kv_w,
            'w_out': w_out, 'ml_w_in': ml_w_in, 'ml_conv_w': ml_conv_w, 'ml_conv_b': ml_conv_b,
            'ml_wq': ml_wq, 'ml_wk': ml_wk, 'ml_wv': ml_wv, 'ml_w_gate': ml_w_gate, 'ml_b_gate': ml_b_gate,
            'ml_mhn_g': ml_mhn_g, 'ml_skip': ml_skip, 'rw_w_in': rw_w_in, 'rw_mu': rw_mu,
            'rw_w_lora2': rw_w_lora2, 'rw_w0': rw_w0, 'rw_a_lora2': rw_a_lora2, 'rw_a0': rw_a0,
            'rw_v_lora2': rw_v_lora2, 'rw_v0': rw_v0, 'rw_g_lora2': rw_g_lora2, 'rw_k_k': rw_k_k,
            'rw_k_a': rw_k_a, 'rw_r_k': rw_r_k, 'rw_lnx_g': rw_lnx_g, 'rw_lnx_b': rw_lnx_b,
            'final_g': final_g}


def reference(x, mem, norm_g, mem_norm_g, mem_kv_w, w_out, ml_w_in, ml_conv_w, ml_conv_b, ml_wq, ml_wk,
              ml_wv, ml_w_gate, ml_b_gate, ml_mhn_g, ml_skip, rw_w_in, rw_mu, rw_w_lora2, rw_w0,
              rw_a_lora2, rw_a0, rw_v_lora2, rw_v0, rw_g_lora2, rw_k_k, rw_k_a, rw_r_k, rw_lnx_g,
              rw_lnx_b, final_g):
    v_first = None
    for i in range(DEPTH):
        j = i // 2
        h = rms_norm(x, norm_g[i])
        mem_n = rms_norm(mem, mem_norm_g[i])
        if i % 2 == 0:
            proj = h @ ml_w_in[j]
            u = proj[..., :D_MIX]
            qm = proj[..., D_MIX:D_MIX + D_XATTN]
            z = proj[..., D_MIX + D_XATTN:]
            y_mix, v_l = mlstm_mixer(u, ml_conv_w[j], ml_conv_b[j], ml_wq[j], ml_wk[j], ml_wv[j],
                                     ml_w_gate[j], ml_b_gate[j], ml_mhn_g[j], ml_skip[j])
            if i == 0:
                v_first = v_l
        else:
            proj = h @ rw_w_in[j]
            p = token_shift_mix(proj[..., :RW_SHIFT_W], rw_mu[j])
            qm = proj[..., RW_SHIFT_W:RW_SHIFT_W + D_XATTN]
            z = proj[..., RW_SHIFT_W + D_XATTN:]
            y_mix = rwkv7_mixer(p, v_first, rw_w_lora2[j], rw_w0[j], rw_a_lora2[j], rw_a0[j],
                                rw_v_lora2[j], rw_v0[j], rw_g_lora2[j], rw_k_k[j], rw_k_a[j],
                                rw_r_k[j], rw_lnx_g[j], rw_lnx_b[j])
        y_mem = memory_attention(qm, mem_n, mem_kv_w[i])
        y = jnp.concatenate([y_mix, y_mem], axis=-1) * jax.nn.silu(z.astype(jnp.float32))
        x = x + y.astype(x.dtype) @ w_out[i]
    return rms_norm(x, final_g)
```

```python
import numpy as np
from contextlib import ExitStack
import concourse.bass as bass
import concourse.mybir as mybir
from concourse.bass_utils import run_bass_kernel_spmd

F32 = mybir.dt.float32
BF16 = mybir.dt.bfloat16
I32 = mybir.dt.int32
AF = mybir.ActivationFunctionType
ALU = mybir.AluOpType
AX = mybir.AxisListType

S = 2048
D = 1024
NT = S // 128
DMIX = 1536
DX = 512
NMEM = 256


class KB:
    def __init__(self):
        self.nc = bass.Bass("TRN2", target_bir_lowering=False)
        nc = self.nc
        self.es = ExitStack()
        self.engs = {'pe': nc.tensor, 'act': nc.scalar, 'dve': nc.vector, 'pool': nc.gpsimd, 'sp': nc.sync}
        self.sems = {}
        for e in ['pe', 'act', 'dve', 'pool']:
            self.sems[e] = self.es.enter_context(nc.semaphore('c_' + e))
        self.cnt = {e: 0 for e in ['pe', 'act', 'dve', 'pool']}
        self.waited = {e: {} for e in self.engs}
        self.dq = {}
        for q, n in [('sp', 20), ('pool', 12), ('act', 6)]:
            for i in range(n):
                self.sems[(q, i)] = self.es.enter_context(nc.semaphore('d_%s%d' % (q, i)))
            self.dq[q] = [n, 0]
        self.dval = {}
        self.res = {}
        self.nbank = 0

    def sb(self, name, shape, dtype, stack=None):
        return (stack or self.es).enter_context(self.nc.sbuf_tensor('s_' + name, shape, dtype))

    def din(self, name, shape, dtype=F32):
        return self.nc.dram_tensor(name, shape, dtype, kind="ExternalInput").ap()

    def dout(self, name, shape, dtype=F32):
        return self.nc.dram_tensor(name, shape, dtype, kind="ExternalOutput").ap()

    def dint(self, name, shape, dtype=F32):
        return self.nc.dram_tensor(name, shape, dtype, kind="Internal").ap()

    def _wait(self, eng, key, val):
        if self.waited[eng].get(key, 0) >= val:
            return
        self.engs[eng].wait_ge(self.sems[key], val)
        self.waited[eng][key] = val

    def _deps(self, R, W):
        deps = {}

        def add(m):
            if m is None:
                return
            k, v = m
            if deps.get(k, 0) < v:
                deps[k] = v
        for r in R:
            st = self.res.get(r)
            if st:
                add(st[0])
        for w in W:
            st = self.res.get(w)
            if st:
                add(st[0])
                for k, v in st[1].items():
                    add((k, v))
        return deps

    def _mark(self, mark, R, W):
        for r in R:
            st = self.res.setdefault(r, [None, {}])
            if st[1].get(mark[0], 0) < mark[1]:
                st[1][mark[0]] = mark[1]
        for w in W:
            self.res[w] = [mark, {}]

    def op(self, eng, fn, R=(), W=()):
        deps = self._deps(R, W)
        for k, v in deps.items():
            if eng == 'pe' and k == 'pe':
                continue
            self._wait(eng, k, v)
        inst = fn(self.engs[eng])
        self.cnt[eng] += 1
        inst.then_inc(self.sems[eng], 1)
        self._mark((eng, self.cnt[eng]), R, W)
        return inst

    def dma(self, q, out, in_, R=(), W=()):
        n, i = self.dq[q]
        self.dq[q][1] = i + 1
        key = (q, i % n)
        gen = i // n
        if gen > 0:
            self._wait(q, key, 16 * gen)
        deps = self._deps(R, W)
        for k, v in deps.items():
            self._wait(q, k, v)
        self.engs[q].dma_start(out=out, in_=in_).then_inc(self.sems[key], 16)
        self.dval[key] = 16 * (gen + 1)
        self._mark((key, 16 * (gen + 1)), R, W)

    def barrier(self):
        for e in self.engs:
            for k in ['pe', 'act', 'dve', 'pool']:
                if self.cnt[k] > 0:
                    self._wait(e, k, self.cnt[k])
            for key, v in self.dval.items():
                self._wait(e, key, v)

    def banks(self, n):
        if self.nbank + n > 8:
            self.nbank = 0
        b = self.nbank
        self.nbank += n
        return b, [('ps', j) for j in range(b, b + n)]

    def _pe_mode(self, lhsT, ser):
        r = lambda n: 32 if n <= 32 else (64 if n <= 64 else 128)
        shp = lhsT.shape
        k = int(shp[0])
        m = 1
        for d in shp[1:]:
            m *= int(d)
        mode = (r(k), r(m), int(lhsT.offset) if False else 0)
        if (ser or mode != getattr(self, 'pe_mode', None)) and self.cnt['pe'] > 0:
            self._wait('pe', 'pe', self.cnt['pe'])
        self.pe_mode = mode

    def mm(self, out, lhsT, rhs, start, stop, R, W, ser=False):
        self._pe_mode(lhsT, ser)
        return self.op('pe', lambda e: e.matmul(out, lhsT, rhs, start=start, stop=stop), R, W)

    def tr(self, out, in_, ident, R, W):
        self._pe_mode(in_, False)
        return self.op('pe', lambda e: e.transpose(out, in_, ident), R, W)

    def act(self, out, in_, func, R, W, eng='act', **kw):
        return self.op('act', lambda e: e.activation(out=out, in_=in_, func=func, **kw), R, W)

    def tt(self, eng, out, in0, in1, op, R, W):
        return self.op(eng, lambda e: e.tensor_tensor(out=out, in0=in0, in1=in1, op=op), R, W)

    def ts(self, eng, out, in0, s1, s2, op0, op1, R, W, **kw):
        if op1 is None:
            return self.op(eng, lambda e: e.tensor_scalar(out=out, in0=in0, scalar1=s1, scalar2=None, op0=op0, **kw), R, W)
        return self.op(eng, lambda e: e.tensor_scalar(out=out, in0=in0, scalar1=s1, scalar2=s2, op0=op0, op1=op1, **kw), R, W)

    def stt(self, out, in0, scalar, in1, op0, op1, R, W):
        return self.op('dve', lambda e: e.scalar_tensor_tensor(out=out, in0=in0, scalar=scalar, in1=in1, op0=op0, op1=op1), R, W)

    def cp(self, eng, out, in_, R, W):
        if eng == 'act':
            return self.op('act', lambda e: e.copy(out=out, in_=in_), R, W)
        return self.op(eng, lambda e: e.tensor_copy(out=out, in_=in_), R, W)

    def memset(self, eng, ap, val, W):
        return self.op(eng, lambda e: e.memset(ap, val), (), W)


def build(stage=2):
    kb = KB()
    nc = kb.nc
    x_d = kb.din('x', [S, D])
    mem_d = kb.din('mem', [NMEM, D])
    norm_g_d = kb.din('norm_g', [2, D])
    memg_d = kb.din('mem_norm_g', [2, D])
    fing_d = kb.din('final_g', [1, D])
    kv_d = [kb.din('kv0', [D, D]), kb.din('kv1', [D, D])]
    wout_d = [kb.din('wout0', [2048, D]), kb.din('wout1', [2048, D])]
    mlw_d = kb.din('ml_w_in', [D, 4096])
    bd_d = {n: kb.din(n, [128, 12, 128]) for n in ['bdq', 'bdk', 'bdv', 'bdqT', 'bdkT', 'bdvT']}
    cwT_d = kb.din('cwT', [128, 12, 4])
    convb_d = kb.din('conv_b', [1, DMIX])
    wg_d = kb.din('wg', [128, 36, 8])
    bgate_d = kb.din('b_gate', [1, 8])
    mhng_d = kb.din('mhn_g', [1, DMIX])
    skipT_d = kb.din('skipT', [128, 12])
    out_d = kb.dout('out', [S, D])
    xres_d = kb.dint('xres', [S, D])
    sz_d = kb.dint('sz', [S, 2048], BF16)
    vfirst_d = kb.dint('vfirst', [S, DMIX], BF16)
    uT_d = kb.dint('uT', [12, 128, 3 + S], BF16)
    rww_d = kb.din('rw_w_in', [D, 7456])
    mu_d = kb.din('rw_mu', [1, 4896])
    wa2_d = kb.din('wa_lora2', [128, DMIX])
    vl2_d = kb.din('v_lora2', [32, DMIX])
    gl2_d = kb.din('g_lora2', [128, DMIX])
    brow_d = kb.din('brows', [3, DMIX])
    reps_d = kb.din('reps', [5, DMIX])
    rkv_d = kb.dint('rkv', [S, 4608], BF16)

    ps = kb.es.enter_context(nc.psum_tensor('ps', [128, 8, 512], F32))
    psb = ps.bitcast(BF16)
    identf = kb.sb('identf', [128, 128], F32)
    identb = kb.sb('identb', [128, 128], BF16)
    tri_incl = kb.sb('tri_incl', [128, 128], F32)
    onesf = kb.sb('onesf', [128, 128], F32)
    onesb = kb.sb('onesb', [1, 128], BF16)
    epsc = kb.sb('epsc', [128, 4], F32)
    featT = kb.sb('featT', [128, 7, 3 + S], BF16)
    wout = kb.sb('wout', [128, 16, D], BF16)
    kmT = kb.sb('kmT', [128, 4, NMEM], BF16)
    vm = kb.sb('vm', [128, 2, DX], BF16)
    grep = kb.sb('grep', [128, D], F32)

    iot = kb.sb('iot', [128, 128], I32)
    if True:
        kb.op('pool', lambda e: e.iota(iot[:], pattern=[[1, 128]], base=0, channel_multiplier=-1), (), ['iot'])
        kb.op('dve', lambda e: e.tensor_single_scalar(out=identf[:], in_=iot[:], scalar=0, op=ALU.is_equal), ['iot'], ['identf'])
        kb.op('dve', lambda e: e.tensor_single_scalar(out=identb[:], in_=iot[:], scalar=0, op=ALU.is_equal), ['iot'], ['identb'])
        kb.op('dve', lambda e: e.tensor_single_scalar(out=tri_incl[:], in_=iot[:], scalar=0, op=ALU.is_ge), ['iot'], ['tri_incl'])
        kb.memset('dve', onesf[:], 1.0, ['onesf'])
        kb.memset('dve', onesb[:], 1.0, ['onesb'])
        kb.memset('dve', epsc[:, 0:1], 1e-6, ['epsc'])
        kb.memset('dve', epsc[:, 1:2], 1e-5, ['epsc'])
        kb.memset('dve', epsc[:, 2:3], 64e-5, ['epsc'])
        kb.memset('dve', epsc[:, 3:4], 1.0, ['epsc'])
        zt = kb.sb('zt', [128, 12, 4], BF16)
        kb.memset('pool', zt[:], 0.0, ['zt'])
        kb.dma('pool', uT_d[:, :, 0:3].rearrange("c p t -> p c t"), zt[:, :, 0:3], ['zt'], ['uT_z'])
        kb.barrier()

    def phase_A(src, ntiles, grow, hT, col0, st, tag):
        kb.dma('sp', grep[:], grow.to_broadcast([128, D]), (), ['grep'])
        xb = [kb.sb('%s_x%d' % (tag, i), [128, D], F32, st) for i in range(2)]
        junk = kb.sb(tag + '_junk', [128, D], BF16, st)
        hs = [kb.sb('%s_hs%d' % (tag, i), [128, D], BF16, st) for i in range(2)]
        ss = kb.sb(tag + '_ss', [128, 8], F32, st)
        if col0 > 0:
            kb.memset('pool', hT[:, :, 0:col0], 0.0, ['hT_z'])
        for i in range(ntiles):
            xt = xb[i % 2]
            rx = 'A_x%d' % (i % 2)
            rss = ('A_ss', i % 2)
            kb.dma('sp', xt[:], src[i * 128:(i + 1) * 128, :], (), [rx])
            c = (i % 2) * 4
            kb.act(junk[:], xt[:], AF.Square, [rx], ['A_junk', rss], accum_out=ss[:, c:c + 1])
            kb.act(ss[:, c + 1:c + 2], ss[:, c:c + 1], AF.Sqrt, [rss, 'epsc'], [rss], scale=1.0 / D, bias=epsc[:, 0:1])
            kb.op('dve', lambda e: e.reciprocal(out=ss[:, c + 2:c + 3], in_=ss[:, c + 1:c + 2]), [rss], [rss])
            h = hs[i % 2]
            rh = 'A_hs%d' % (i % 2)
            kb.stt(h[:], xt[:], ss[:, c + 2:c + 3], grep[:], ALU.mult, ALU.mult, [rx, rss, 'grep'], [rh])
            b, rb = kb.banks(1)
            for k in range(8):
                kb.tr(psb[:, b, k * 128:(k + 1) * 128], h[:, k * 128:(k + 1) * 128], identb[:], [rh, 'identb'], rb)
            kb.cp('act' if i % 2 else 'dve', hT[:, :, col0 + i * 128: col0 + (i + 1) * 128],
                  psb[:, b, :].rearrange("p (k t) -> p k t", k=8), rb, [('hT', i)])

    def load_w_block(Wd, c0, ncols, wb, rw):
        kb.dma('pool', wb[:, :, 0:ncols], Wd[:, c0:c0 + ncols].rearrange("(k p) c -> p k c", p=128), (), [rw])

    def proj_T(hT, col0, ti, hres, wb, rw, ncols, wb2=None, rw2=None):
        b, rb = kb.banks(1)
        n = 16 if wb2 is not None else 8
        j = 0
        for k in range(8):
            kb.mm(ps[:, b, 0:ncols], hT[:, k, col0 + ti * 128: col0 + (ti + 1) * 128], wb[:, k, 0:ncols], j == 0, j == n - 1, list(hres) + [rw], rb)
            j += 1
        if wb2 is not None:
            for k in range(8):
                kb.mm(ps[:, b, 0:ncols], hT[:, k, col0 - 1 + ti * 128: col0 - 1 + (ti + 1) * 128], wb2[:, k, 0:ncols], False, j == n - 1, list(hres) + [rw2], rb)
                j += 1
        return b, rb

    def proj_F(hT, col0, tb, ntok, hres, wb, rw, m0, M, wb2=None, rw2=None):
        b, rb = kb.banks(1)
        n = 16 if wb2 is not None else 8
        j = 0
        for k in range(8):
            kb.mm(ps[0:M, b, 0:ntok], wb[:, k, m0:m0 + M], hT[:, k, col0 + tb: col0 + tb + ntok], j == 0, j == n - 1, list(hres) + [rw], rb)
            j += 1
        if wb2 is not None:
            for k in range(8):
                kb.mm(ps[0:M, b, 0:ntok], wb2[:, k, m0:m0 + M], hT[:, k, col0 - 1 + tb: col0 - 1 + tb + ntok], False, j == n - 1, list(hres) + [rw2], rb)
                j += 1
        return b, rb

    def mem_kv(layer, st):
        memT = kb.sb('memT%d' % layer, [128, 8, NMEM], BF16, st)
        phase_A(mem_d, 2, memg_d[layer:layer + 1, :], memT, 0, st, 'Am%d' % layer)
        wbk = [kb.sb('kvw%d_%d' % (layer, i), [128, 8, 512], BF16, st) for i in range(2)]
        for i in range(2):
            load_w_block(kv_d[layer], i * 512, 512, wbk[i], 'kvw%d' % i)
        hres = [('hT', 0), ('hT', 1)]
        for h in range(4):
            b, rb = kb.banks(1)
            for k in range(8):
                kb.mm(ps[:, b, 0:NMEM], wbk[0][:, k, h * 128:(h + 1) * 128], memT[:, k, :], k == 0, k == 7, hres + ['kvw0'], rb)
            kb.cp('act', kmT[:, h, :], ps[:, b, 0:NMEM], rb, ['kmT'])
        for mt in range(2):
            b, rb = kb.banks(1)
            for k in range(8):
                kb.mm(ps[:, b, 0:512], memT[:, k, mt * 128:(mt + 1) * 128], wbk[1][:, k, :], k == 0, k == 7, hres + ['kvw1'], rb)
            kb.cp('act', vm[:, mt, :], ps[:, b, 0:512], rb, ['vm'])

    def attn_and_out(ti, qcol, y, ry, xt, rx, st_bufs, dst_d, final=False, yx=()):
        sc, pbuf, pT, yT = st_bufs
        b, rb = kb.banks(2)
        for h in range(4):
            kb.mm(ps[:, b + h // 2, (h % 2) * 256:(h % 2) * 256 + 256], featT[:, qcol + h, 3 + ti * 128: 3 + (ti + 1) * 128], kmT[:, h, :], True, True,
                  [('featT', ti), 'kmT'], [rb[h // 2]])
        kb.op('dve', lambda e: e.tensor_reduce(out=sc[:, 0:4], in_=ps[:, b:b + 2, :].rearrange("p b (h m) -> p (b h) m", h=2), axis=AX.X, op=ALU.max), rb, ['at_sc'])
        kb.ts('dve', sc[:, 4:8], sc[:, 0:4], -(128.0 ** -0.5), None, ALU.mult, None, ['at_sc'], ['at_sc'])
        for h in range(4):
            kb.act(pbuf[:, h, :], ps[:, b + h // 2, (h % 2) * 256:(h % 2) * 256 + 256], AF.Exp, [rb[h // 2], 'at_sc'], ['at_p', ('at_sum', h)],
                   scale=128.0 ** -0.5, bias=sc[:, 4 + h:5 + h], accum_out=sc[:, 8 + h:9 + h])
        kb.op('dve', lambda e: e.reciprocal(out=sc[:, 12:16], in_=sc[:, 8:12]), [('at_sum', h) for h in range(4)], ['at_rinv'])
        b2, rb2 = kb.banks(1)
        for h in range(4):
            for mc in range(2):
                kb.tr(psb[:, b2, (h * 2 + mc) * 128:(h * 2 + mc + 1) * 128], pbuf[:, h, mc * 128:(mc + 1) * 128], identb[:], ['at_p', 'identb'], rb2)
        kb.cp('act', pT[:], psb[:, b2, :], rb2, ['at_pT'])
        b3, rb3 = kb.banks(1)
        for h in range(4):
            for mc in range(2):
                kb.mm(ps[:, b3, h * 128:(h + 1) * 128], pT[:, (h * 2 + mc) * 128:(h * 2 + mc + 1) * 128], vm[:, mc, h * 128:(h + 1) * 128], mc == 0, mc == 1, ['at_pT', 'vm'], rb3)
        for h in range(4):
            kb.stt(y[:, DMIX + h * 128: DMIX + (h + 1) * 128], ps[:, b3, h * 128:(h + 1) * 128], sc[:, 12 + h:13 + h], y[:, DMIX + h * 128: DMIX + (h + 1) * 128],
                   ALU.mult, ALU.mult, rb3 + ['at_rinv', ry], [ry])
        b4, rb4 = kb.banks(2)
        for c in range(16):
            kb.tr(psb[:, b4 + c // 8, (c % 8) * 128:(c % 8 + 1) * 128], y[:, c * 128:(c + 1) * 128], identb[:], [ry, 'identb'], [rb4[c // 8]])
        kb.cp('act', yT[:, 0:8, :], psb[:, b4, :].rearrange("p (c t) -> p c t", c=8), [rb4[0]], ['yT0'] + list(yx))
        kb.cp('dve', yT[:, 8:16, :], psb[:, b4 + 1, :].rearrange("p (c t) -> p c t", c=8), [rb4[1]], ['yT1'] + list(yx))
        b5, rb5 = kb.banks(2)
        for nb in range(2):
            for c in range(16):
                kb.mm(ps[:, b5 + nb, :], yT[:, c, :], wout[:, c, nb * 512:(nb + 1) * 512], c == 0, c == 15, ['yT0', 'yT1', 'wout'], [rb5[nb]])
        kb.tt('dve', xt[:], ps[:, b5:b5 + 2, :].rearrange("p b n -> p (b n)"), xt[:], ALU.add, rb5 + [rx], [rx])
        if final:
            kb.act(pT[:], xt[:], AF.Square, [rx], ['at_pT', 'fin_ss'], accum_out=sc[:, 16:17])
            kb.act(sc[:, 17:18], sc[:, 16:17], AF.Sqrt, ['fin_ss', 'epsc'], ['fin_ss'], scale=1.0 / D, bias=epsc[:, 0:1])
            kb.op('dve', lambda e: e.reciprocal(out=sc[:, 18:19], in_=sc[:, 17:18]), ['fin_ss'], ['fin_ss'])
            kb.stt(xt[:], xt[:], sc[:, 18:19], grep[:], ALU.mult, ALU.mult, [rx, 'fin_ss', 'grep'], [rx])
        kb.dma('pool', dst_d[ti * 128:(ti + 1) * 128, :], xt[:], [rx], [('xres', ti)])

    with ExitStack() as st:
        mem_kv(0, st)
        kb.barrier()
    with ExitStack() as st:
        hT = kb.sb('hT0', [128, 8, S + 1], BF16, st)
        phase_A(x_d, NT, norm_g_d[0:1, :], hT, 1, st, 'A0')
        hres_all = [('hT', i) for i in range(NT)]
        for c4 in range(4):
            kb.dma('pool', wout[:, c4 * 4:(c4 + 1) * 4, :], wout_d[0][c4 * 512:(c4 + 1) * 512, :].rearrange("(c p) n -> p c n", p=128), (), ['wout'])
        wbs = [kb.sb('wb%d' % i, [128, 8, 512], BF16, st) for i in range(2)]
        stg = [kb.sb('stg%d' % i, [128, 512], BF16, st) for i in range(3)]
        nstg = 0
        for blk in range(8):
            wb = wbs[blk % 2]
            rw = 'wb%d' % (blk % 2)
            load_w_block(mlw_d, blk * 512, 512, wb, rw)
            if blk < 4:
                for m in range(4):
                    ch = blk * 4 + m
                    for tb in range(4):
                        b, rb = proj_F(hT, 1, tb * 512, 512, hres_all, wb, rw, m * 128, 128)
                        if blk < 3:
                            sg = stg[nstg % 3]
                            rs = 'stg%d' % (nstg % 3)
                            nstg += 1
                            kb.cp('act' if tb % 2 else 'dve', sg[:], ps[:, b, :], rb, [rs])
                            kb.dma('pool', uT_d[ch, :, 3 + tb * 512: 3 + (tb + 1) * 512], sg[:], [rs], [('uT', tb * 4 + q) for q in range(4)])
                        else:
                            kb.cp('act' if tb % 2 else 'dve', featT[:, m, 3 + tb * 512: 3 + (tb + 1) * 512], ps[:, b, :], rb,
                                  [('featT', tb * 4 + q) for q in range(4)])
            else:
                for ti in range(NT):
                    b, rb = proj_T(hT, 1, ti, [('hT', ti)], wb, rw, 512)
                    sg = stg[nstg % 3]
                    rs = 'stg%d' % (nstg % 3)
                    nstg += 1
                    kb.act(sg[:], ps[:, b, :], AF.Silu, rb, [rs])
                    kb.dma('pool', sz_d[ti * 128:(ti + 1) * 128, (blk - 4) * 512:(blk - 3) * 512], sg[:], [rs], [('sz', ti)])
        kb.barrier()

    with ExitStack() as st:
        Cf = kb.sb('Cf', [128, 12, 385], F32, st)
        Cb = kb.sb('Cb', [128, 12, 385], BF16, st)
        BD = {n: kb.sb('s_' + n, [128, 12, 128], BF16, st) for n in ['bdq', 'bdk', 'bdv']}
        convD = kb.sb('convD', [128, 4, 12, 128], BF16, st)
        diagS = kb.sb('diagS', [128, 12, 128], BF16, st)
        cwT = kb.sb('cwT', [128, 12, 4], F32, st)
        skT = kb.sb('skT', [128, 12], F32, st)
        convb = kb.sb('convb', [1, DMIX], BF16, st)
        Gqk = kb.sb('Gqk', [128, 12, 8], BF16, st)
        Gv = kb.sb('Gv', [128, 12, 8], BF16, st)
        bgate = kb.sb('bgate', [1, 8], BF16, st)
        mhn_rep = kb.sb('mhn_rep', [128, DMIX], BF16, st)
        kb.memset('dve', Cf[:], 0.0, ['Cf'])
        kb.memset('pool', Cb[:], 0.0, ['Cb'])
        for n in ['bdq', 'bdk', 'bdv']:
            kb.dma('pool', BD[n][:], bd_d[n], (), [n])
        kb.dma('sp', cwT[:], cwT_d, (), ['cwT'])
        kb.dma('sp', skT[:], skipT_d, (), ['skT'])
        kb.dma('pool', convb[:], convb_d, (), ['convb'])
        kb.dma('pool', bgate[:], bgate_d, (), ['bgate'])
        kb.dma('pool', mhn_rep[:], mhng_d.to_broadcast([128, DMIX]), (), ['mhn_rep'])
        for kk in range(4):
            for c in range(12):
                kb.ts('dve' if c % 2 else 'pool', convD[:, kk, c, :], identf[:], cwT[:, c, kk:kk + 1], None, ALU.mult, None, ['identf', 'cwT'], ['convD'])
        for c in range(12):
            kb.ts('dve' if c % 2 else 'pool', diagS[:, c, :], identf[:], skT[:, c:c + 1], None, ALU.mult, None, ['identf', 'skT'], ['diagS'])
        with ExitStack() as st2:
            bdT = [kb.sb('bdT%d' % i, [128, 12, 128], F32, st2) for i in range(3)]
            wgf = kb.sb('wgf', [128, 36, 8], F32, st2)
            kb.dma('sp', wgf[:], wg_d, (), ['wgf'])
            for gi, n in enumerate(['bdqT', 'bdkT', 'bdvT']):
                kb.dma('sp', bdT[gi][:], bd_d[n], (), ['bdT%d' % gi])
            b, rb = kb.banks(1)
            for c in range(12):
                kb.mm(ps[:, b, c * 8:c * 8 + 8], bdT[0][:, c, :], wgf[:, c, :], True, False, ['bdT0', 'wgf'], rb)
                kb.mm(ps[:, b, c * 8:c * 8 + 8], bdT[1][:, c, :], wgf[:, 12 + c, :], False, True, ['bdT1', 'wgf'], rb)
            for c in range(12):
                kb.mm(ps[:, b, 96 + c * 8:96 + c * 8 + 8], bdT[2][:, c, :], wgf[:, 24 + c, :], True, True, ['bdT2', 'wgf'], rb)
            kb.cp('dve', Gqk[:], ps[:, b, 0:96].rearrange("p (c g) -> p c g", c=12), rb, ['Gqk'])
            kb.cp('dve', Gv[:], ps[:, b, 96:192].rearrange("p (c g) -> p c g", c=12), rb, ['Gv'])
            kb.barrier()

        NB = 2
        uTb = [kb.sb('uTt%d' % i, [128, 12, 131], BF16, st) for i in range(NB)]
        szb = [kb.sb('szt%d' % i, [128, 2048], BF16, st) for i in range(NB)]
        xtb = [kb.sb('xt%d' % i, [128, D], F32, st) for i in range(NB)]
        xcT = kb.sb('xcT', [128, 12, 128], BF16, st)
        qT = kb.sb('qT', [128, 12, 128], BF16, st)
        kT = kb.sb('kT', [128, 12, 128], BF16, st)
        kw = kb.sb('kw', [128, 4, 384], BF16, st)
        vx = kb.sb('vx', [128, 4, 385], BF16, st)
        PT = kb.sb('PT', [128, 4, 128], BF16, st)
        gs = [kb.sb('gs%d' % i, [128, 64], F32, st) for i in range(NB)]
        hh = kb.sb('hh', [128, DMIX], F32, st)
        bst = kb.sb('bst', [128, 4, 6], F32, st)
        sc = kb.sb('sc', [128, 24], F32, st)
        pbuf = kb.sb('pbuf', [128, 4, NMEM], BF16, st)
        pT = kb.sb('pT', [128, 1024], BF16, st)
        yT = kb.sb('yT', [128, 16, 128], BF16, st)
        kb.memset('pool', vx[:, :, 384:385], 1.0, ['vx1'])

        for ti in range(NT):
            p = ti % NB
            P = lambda n: '%s%d' % (n, p)
            ut = uTb[p]
            ru = P('uTt')
            kb.dma('sp', ut[:], uT_d[:, :, ti * 128: ti * 128 + 131].rearrange("c p t -> p c t"),
                   [('uT', ti), ('uT', max(ti - 1, 0)), 'uT_z'], [ru])
            kb.dma('sp', szb[p][:], sz_d[ti * 128:(ti + 1) * 128, :], [('sz', ti)], [P('szt')])
            kb.dma('sp', xtb[p][:], x_d[ti * 128:(ti + 1) * 128, :], (), [P('xt')])
            b, rb = kb.banks(3)
            for c in range(12):
                o = ps[:, b + c // 4, (c % 4) * 128:(c % 4 + 1) * 128]
                for kk in range(4):
                    kb.mm(o, convD[:, kk, c, :], ut[:, c, kk: kk + 128], kk == 0, False, [ru, 'convD'], [rb[c // 4]])
                kb.mm(o, convb[0:1, c * 128:(c + 1) * 128], onesb[0:1, :], False, True, ['convb', 'onesb'], [rb[c // 4]])
            kb.act(xcT[:].rearrange("p c t -> p (c t)"), ps[:, b:b + 3, :].rearrange("p b n -> p (b n)"), AF.Silu, rb, ['xcT'])
            bg, rbg = kb.banks(1)
            for c in range(12):
                kb.mm(ps[:, bg, 0:8], xcT[:, c, :], Gqk[:, c, :], c == 0, False, ['xcT', 'Gqk'], rbg)
            for c in range(12):
                kb.mm(ps[:, bg, 0:8], ut[:, c, 3:131], Gv[:, c, :], False, False, [ru, 'Gv'], rbg)
            kb.mm(ps[:, bg, 0:8], onesb[0:1, :], bgate[0:1, :], False, True, ['onesb', 'bgate'], rbg)
            g = gs[p]
            rg = P('gs')
            kb.cp('dve', g[:, 0:4], ps[:, bg, 0:4], rbg, [rg])
            kb.act(g[:, 4:8], ps[:, bg, 4:8], AF.Exp, rbg, [rg], scale=-1.0)
            kb.act(g[:, 8:12], g[:, 4:8], AF.Ln, [rg, 'epsc'], [rg], bias=epsc[:, 3:4])
            kb.ts('dve', g[:, 8:12], g[:, 8:12], -1.0, None, ALU.mult, None, [rg], [rg])
            b2, rb2 = kb.banks(1)
            kb.mm(ps[:, b2, 0:4], tri_incl[:], g[:, 8:12], True, True, ['tri_incl', rg], rb2)
            kb.mm(ps[:, b2, 4:8], onesf[:], g[:, 8:12], True, True, ['onesf', rg], rb2)
            kb.cp('dve', g[:, 12:20], ps[:, b2, 0:8], rb2, [rg])
            kb.tt('dve', g[:, 32:36], g[:, 0:4], g[:, 12:16], ALU.subtract, [rg], [rg])
            kb.act(g[:, 20:24], g[:, 32:36], AF.Exp, [rg], [rg])
            kb.act(g[:, 24:32], g[:, 12:20], AF.Exp, [rg], [rg])
            b, rb = kb.banks(3)
            for c in range(12):
                kb.mm(ps[:, b + c // 4, (c % 4) * 128:(c % 4 + 1) * 128], BD['bdq'][:, c, :], xcT[:, c, :], True, True, ['xcT', 'bdq'], [rb[c // 4]])
            kb.cp('act', qT[:].rearrange("p c t -> p (c t)"), ps[:, b:b + 3, :].rearrange("p b n -> p (b n)"), rb, ['qT'])
            b, rb = kb.banks(3)
            for c in range(12):
                kb.mm(ps[:, b + c // 4, (c % 4) * 128:(c % 4 + 1) * 128], BD['bdk'][:, c, :], xcT[:, c, :], True, True, ['xcT', 'bdk'], [rb[c // 4]])
            kb.act(kT[:].rearrange("p c t -> p (c t)"), ps[:, b:b + 3, :].rearrange("p b n -> p (b n)"), AF.Copy, rb, ['kT'], scale=384.0 ** -0.5)
            b, rb = kb.banks(3)
            for c in range(12):
                kb.mm(ps[:, b + c // 4, (c % 4) * 128:(c % 4 + 1) * 128], xcT[:, c, :], BD['bdk'][:, c, :], True, True, ['xcT', 'bdk'], [rb[c // 4]])
            for h in range(4):
                kb.ts('dve', kw[:, h, :], ps[:, b:b + 3, :].rearrange("p b n -> p (b n)")[:, h * 384:(h + 1) * 384], g[:, 20 + h:21 + h], 384.0 ** -0.5,
                      ALU.mult, ALU.mult, rb + [rg], ['kw'])
            b, rb = kb.banks(3)
            for c in range(12):
                kb.mm(ps[:, b + c // 4, (c % 4) * 128:(c % 4 + 1) * 128], ut[:, c, 3:131], BD['bdv'][:, c, :], True, True, [ru, 'bdv'], [rb[c // 4]])
            kb.cp('act', vx[:, :, 0:384], ps[:, b:b + 3, :].rearrange("p b n -> p (b n)").rearrange("p (h e) -> p h e", h=4), rb, ['vx'])
            kb.dma('pool', vfirst_d[ti * 128:(ti + 1) * 128, :].rearrange("t (h e) -> t h e", h=4), vx[:, :, 0:384], ['vx'], [('vfirst', ti)])
            b, rb = kb.banks(1)
            for h in range(4):
                for dc in range(3):
                    kb.mm(ps[:, b, h * 128:(h + 1) * 128], kT[:, h * 3 + dc, :], qT[:, h * 3 + dc, :], dc == 0, dc == 2, ['kT', 'qT'], rb)
            for h in range(4):
                kb.stt(PT[:, h, :], ps[:, b, h * 128:(h + 1) * 128], g[:, 20 + h:21 + h], tri_incl[:], ALU.mult, ALU.mult, rb + [rg, 'tri_incl'], ['PT'])
            bn, rbn = kb.banks(4)
            for h in range(4):
                kb.mm(ps[:, bn + h, 0:385], PT[:, h, :], vx[:, h, :], True, False, ['PT', 'vx', 'vx1'], [rbn[h]])
                for dc in range(3):
                    kb.mm(ps[:, bn + h, 0:385], qT[:, h * 3 + dc, :], Cb[:, h * 3 + dc, :], False, dc == 2, ['qT', 'Cb'], [rbn[h]])
            kb.tt('dve', g[:, 36:40], ps[:, bn:bn + 4, 384], g[:, 24:28], ALU.mult, rbn + [rg], [rg])
            kb.act(g[:, 36:40], g[:, 36:40], AF.Abs, [rg], [rg])
            kb.ts('dve', g[:, 36:40], g[:, 36:40], 1.0, None, ALU.max, None, [rg], [rg])
            kb.op('dve', lambda e: e.reciprocal(out=g[:, 40:44], in_=g[:, 36:40]), [rg], [rg])
            kb.tt('dve', g[:, 44:48], g[:, 40:44], g[:, 24:28], ALU.mult, [rg], [rg])
            for h in range(4):
                kb.act(hh[:, h * 384:(h + 1) * 384], ps[:, bn + h, 0:384], AF.Copy, [rbn[h], rg], ['hh'], scale=g[:, 44 + h:45 + h])
            for h in range(4):
                bs, rbs = kb.banks(3)
                for dc in range(3):
                    kb.mm(ps[:, bs + dc, 0:385], kw[:, h, dc * 128:(dc + 1) * 128], vx[:, h, :], True, True, ['kw', 'vx', 'vx1'], [rbs[dc]])
                kb.tt('dve', Cf[:, h * 3:h * 3 + 3, :], ps[:, bs:bs + 3, 0:385], Cf[:, h * 3:h * 3 + 3, :], ALU.add, rbs + ['Cf'], ['Cf'])
                kb.ts('pool', Cf[:, h * 3:h * 3 + 3, :], Cf[:, h * 3:h * 3 + 3, :], g[:, 28 + h:29 + h], None, ALU.mult, None, ['Cf', rg], ['Cf'])
                kb.cp('act', Cb[:, h * 3:h * 3 + 3, :], Cf[:, h * 3:h * 3 + 3, :], ['Cf'], ['Cb'])
            for h in range(4):
                kb.op('dve', lambda e: e.bn_stats(out=bst[:, h, :], in_=hh[:, h * 384:(h + 1) * 384]), ['hh'], ['bst'])
            for h in range(4):
                kb.op('dve', lambda e: e.bn_aggr(out=g[:, 48 + 2 * h:50 + 2 * h], in_=bst[:, h, :]), ['bst'], [rg])
            for h in range(4):
                kb.act(g[:, 56 + h:57 + h], g[:, 49 + 2 * h:50 + 2 * h], AF.Sqrt, [rg, 'epsc'], [rg], bias=epsc[:, 1:2])
            kb.op('dve', lambda e: e.reciprocal(out=g[:, 60:64], in_=g[:, 56:60]), [rg], [rg])
            for h in range(4):
                kb.ts('dve', hh[:, h * 384:(h + 1) * 384], hh[:, h * 384:(h + 1) * 384], g[:, 48 + 2 * h:49 + 2 * h], g[:, 60 + h:61 + h],
                      ALU.subtract, ALU.mult, ['hh', rg], ['hh'])
            kb.tt('pool', hh[:], hh[:], mhn_rep[:], ALU.mult, ['hh', 'mhn_rep'], ['hh'])
            b, rb = kb.banks(3)
            for c in range(12):
                kb.mm(ps[:, b + c // 4, (c % 4) * 128:(c % 4 + 1) * 128], xcT[:, c, :], diagS[:, c, :], True, True, ['xcT', 'diagS'], [rb[c // 4]])
            kb.tt('dve', hh[:], ps[:, b:b + 3, :].rearrange("p b n -> p (b n)"), hh[:], ALU.add, rb + ['hh'], ['hh'])
            kb.tt('pool', szb[p][:, 0:DMIX], hh[:], szb[p][:, 0:DMIX], ALU.mult, ['hh', P('szt')], [P('szt')])
            attn_and_out(ti, 0, szb[p], P('szt'), xtb[p], P('xt'), (sc, pbuf, pT, yT), xres_d if stage > 1 else out_d, final=False)
        kb.barrier()


    if stage < 2:
        kb.barrier()
        return kb

    CDEC = -0.6065306597126334
    with ExitStack() as st:
        mem_kv(1, st)
        kb.barrier()
    with ExitStack() as st:
        hT = kb.sb('hT1', [128, 8, S + 1], BF16, st)
        phase_A(xres_d, NT, norm_g_d[1:2, :], hT, 1, st, 'A1')
        hres_all = [('hT', i) for i in range(NT)] + ['hT_z']
        for c4 in range(4):
            kb.dma('pool', wout[:, c4 * 4:(c4 + 1) * 4, :], wout_d[1][c4 * 512:(c4 + 1) * 512, :].rearrange("(c p) n -> p c n", p=128), (), ['wout'])
        wst = kb.sb('wst', [128, 8, 512], F32, st)
        mur = kb.sb('mur', [128, 512], F32, st)
        omu = kb.sb('omu', [128, 512], F32, st)
        wbs = [kb.sb('r_wb%d' % i, [128, 8, 512], BF16, st) for i in range(2)]
        wbs2 = [kb.sb('r_wc%d' % i, [128, 8, 512], BF16, st) for i in range(2)]
        stg = [kb.sb('r_stg%d' % i, [128, 512], BF16, st) for i in range(3)]
        nstg = 0

        def load_shifted(c0, ncols, par):
            kb.dma('sp', wst[:, :, 0:ncols], rww_d[:, c0:c0 + ncols].rearrange("(k p) c -> p k c", p=128), (), ['wst'])
            kb.dma('sp', mur[:, 0:ncols], mu_d[0:1, c0:c0 + ncols].to_broadcast([128, ncols]), (), ['mur'])
            kb.ts('pool', omu[:, 0:ncols], mur[:, 0:ncols], -1.0, 1.0, ALU.mult, ALU.add, ['mur'], ['omu'])
            kb.tt('dve', wbs2[par][:, :, 0:ncols], wst[:, :, 0:ncols], mur[:, 0:ncols].unsqueeze(1).to_broadcast([128, 8, ncols]), ALU.mult,
                  ['wst', 'mur'], ['wc%d' % par])
            kb.tt('pool', wbs[par][:, :, 0:ncols], wst[:, :, 0:ncols], omu[:, 0:ncols].unsqueeze(1).to_broadcast([128, 8, ncols]), ALU.mult,
                  ['wst', 'omu'], ['wb%d' % par])

        nblk = 0
        for blk in range(9):
            par = nblk % 2
            nblk += 1
            load_shifted(blk * 512, 512, par)
            for ti in range(NT):
                b, rb = proj_T(hT, 1, ti, [('hT', ti), ('hT', max(ti - 1, 0)), 'hT_z'], wbs[par], 'wb%d' % par, 512, wbs2[par], 'wc%d' % par)
                sg_ = stg[nstg % 3]
                rs = 'stg%d' % (nstg % 3)
                nstg += 1
                kb.cp('act' if ti % 2 else 'dve', sg_[:], ps[:, b, :], rb, [rs])
                kb.dma('pool', rkv_d[ti * 128:(ti + 1) * 128, blk * 512:(blk + 1) * 512], sg_[:], [rs], [('rkv', ti)])
        par = nblk % 2
        nblk += 1
        load_shifted(4608, 288, par)
        for tb in range(4):
            cs = slice(3 + tb * 512, 3 + (tb + 1) * 512)
            fw = [('featT', tb * 4 + q) for q in range(4)]
            b, rb = proj_F(hT, 1, tb * 512, 512, hres_all, wbs[par], 'wb%d' % par, 0, 128, wbs2[par], 'wc%d' % par)
            kb.act(featT[0:64, 4, cs], ps[0:64, b, :], AF.Tanh, rb, fw)
            kb.cp('dve', featT[64:128, 4, cs], ps[64:128, b, :], rb, fw)
            b, rb = proj_F(hT, 1, tb * 512, 512, hres_all, wbs[par], 'wb%d' % par, 128, 32, wbs2[par], 'wc%d' % par)
            kb.cp('dve', featT[0:32, 5, cs], ps[0:32, b, :], rb, fw)
            b, rb = proj_F(hT, 1, tb * 512, 512, hres_all, wbs[par], 'wb%d' % par, 160, 128, wbs2[par], 'wc%d' % par)
            kb.act(featT[:, 6, cs], ps[:, b, :], AF.Sigmoid, rb, fw)
        par = nblk % 2
        nblk += 1
        load_w_block(rww_d, 4896, 512, wbs[par], 'wb%d' % par)
        for m in range(4):
            for tb in range(4):
                b, rb = proj_F(hT, 1, tb * 512, 512, hres_all, wbs[par], 'wb%d' % par, m * 128, 128)
                kb.cp('act' if tb % 2 else 'dve', featT[:, m, 3 + tb * 512: 3 + (tb + 1) * 512], ps[:, b, :], rb,
                      [('featT', tb * 4 + q) for q in range(4)])
        for zb in range(4):
            par = nblk % 2
            nblk += 1
            load_w_block(rww_d, 5408 + zb * 512, 512, wbs[par], 'wb%d' % par)
            for ti in range(NT):
                b, rb = proj_T(hT, 1, ti, [('hT', ti)], wbs[par], 'wb%d' % par, 512)
                sg_ = stg[nstg % 3]
                rs = 'stg%d' % (nstg % 3)
                nstg += 1
                kb.act(sg_[:], ps[:, b, :], AF.Silu, rb, [rs])
                kb.dma('pool', sz_d[ti * 128:(ti + 1) * 128, zb * 512:(zb + 1) * 512], sg_[:], [rs], [('sz', ti)])
        kb.barrier()

    if stage == 3:
        kb.barrier()
        return kb

    with ExitStack() as st:
        wa2 = kb.sb('wa2', [128, DMIX], BF16, st)
        vl2 = kb.sb('vl2', [32, DMIX], BF16, st)
        gl2 = kb.sb('gl2', [128, DMIX], BF16, st)
        brows = kb.sb('brows', [3, DMIX], BF16, st)
        sel = kb.sb('sel', [3, 3, 128], BF16, st)
        reps = kb.sb('reps', [128, 5, DMIX], BF16, st)
        tsc_incl = kb.sb('tsc_incl', [128, 128], F32, st)
        tsc_strict = kb.sb('tsc_strict', [128, 128], F32, st)
        tsc_rev = kb.sb('tsc_rev', [128, 128], F32, st)
        mST2 = kb.sb('mST2', [128, 4, 128], BF16, st)
        mTS = kb.sb('mTS', [128, 4, 128], BF16, st)
        negcol = kb.sb('negcol', [128, 2], F32, st)
        STf = kb.sb('STf', [64, 24, 64], F32, st)
        STb = kb.sb('STb', [64, 24, 64], BF16, st)
        kb.dma('pool', wa2[:], wa2_d, (), ['wa2'])
        kb.dma('pool', vl2[:], vl2_d, (), ['vl2'])
        kb.dma('pool', gl2[:], gl2_d, (), ['gl2'])
        kb.dma('pool', brows[:], brow_d, (), ['brows'])
        for i in range(5):
            kb.dma('pool', reps[:, i, :], reps_d[i:i + 1, :].to_broadcast([128, DMIX]), (), ['reps'])
        kb.dma('sp', grep[:], fing_d.to_broadcast([128, D]), (), ['grep'])
        for r_ in range(3):
            kb.cp('dve', sel[:, r_, :], identb[0:3, r_:r_ + 1].to_broadcast([3, 128]), ['identb'], ['sel'])
        kb.ts('dve', tsc_incl[:], tri_incl[:], CDEC, None, ALU.mult, None, ['tri_incl'], ['tsc_incl'])
        kb.op('dve', lambda e: e.tensor_single_scalar(out=tsc_strict[:], in_=iot[:], scalar=0, op=ALU.is_gt), ['iot'], ['tsc_strict'])
        kb.ts('dve', tsc_strict[:], tsc_strict[:], CDEC, None, ALU.mult, None, ['tsc_strict'], ['tsc_strict'])
        kb.op('dve', lambda e: e.tensor_single_scalar(out=tsc_rev[:], in_=iot[:], scalar=0, op=ALU.is_lt), ['iot'], ['tsc_rev'])
        kb.ts('dve', tsc_rev[:], tsc_rev[:], CDEC, None, ALU.mult, None, ['tsc_rev'], ['tsc_rev'])
        for q in range(4):
            kb.op('dve', lambda e: e.tensor_single_scalar(out=mST2[:, q, :], in_=iot[:], scalar=0, op=(ALU.is_gt if q % 2 == 0 else ALU.is_ge)), ['iot'], ['mST2'])
            kb.op('dve', lambda e: e.tensor_single_scalar(out=mTS[:, q, :], in_=iot[:], scalar=0, op=ALU.is_lt), ['iot'], ['mTS'])
        kb.memset('dve', negcol[:], CDEC, ['negcol'])
        kb.memset('dve', STf[:], 0.0, ['STf'])
        kb.memset('pool', STb[:], 0.0, ['STb'])

        NB = 1
        szb = [kb.sb('r_szt%d' % i, [128, 2048], BF16, st) for i in range(NB)]
        xtb = [kb.sb('r_xt%d' % i, [128, D], F32, st) for i in range(NB)]
        rkv = kb.sb('r_rkv', [128, 4608], BF16, st)
        sg = kb.sb('r_sg', [128, DMIX], F32, st)
        av = kb.sb('r_av', [128, DMIX], BF16, st)
        gg = kb.sb('r_gg', [128, DMIX], BF16, st)
        V = kb.sb('r_V', [128, DMIX], BF16, st)
        kkn = kb.sb('r_kkn', [128, DMIX], BF16, st)
        kp = kb.sb('r_kp', [128, DMIX], BF16, st)
        bv = kb.sb('r_bv', [128, DMIX], BF16, st)
        bon = kb.sb('r_bon', [128, DMIX], BF16, st)
        ef = kb.sb('r_ef', [128, DMIX], F32, st)
        tmpf = kb.sb('r_tmpf', [128, DMIX], F32, st)
        at = kb.sb('r_at', [128, DMIX], BF16, st)
        xs = kb.sb('r_xs', [128, DMIX], BF16, st)
        vf = xs
        bbar = kkn
        kbar = av
        XT = kb.sb('r_XT', [128, 12, 4, 128], BF16, st)
        sm = kb.sb('r_sm', [128, 128], F32, st)
        WLT = kb.sb('r_WLT', [64, 24], F32, st)
        rt = kb.sb('r_rt', [128, DMIX], BF16, st)
        ATab = kb.sb('r_ATab', [128, 4, 2, 128], BF16, st)
        ATak = kb.sb('r_ATak', [128, 4, 2, 128], BF16, st)
        Qb = [kb.sb('r_Q%d' % i, [128, 4, 128], BF16, st) for i in range(2)]
        Pb = [kb.sb('r_P%d' % i, [128, 4, 128], BF16, st) for i in range(2)]
        TTb = [kb.sb('r_TT%d' % i, [128, 4, 128], BF16, st) for i in range(2)]
        X1s = kb.sb('r_X1s', [128, 4, 64], BF16, st)
        AU = kb.sb('r_AU', [128, 4, 128], BF16, st)
        RT = kb.sb('r_RT', [64, 4, 128], BF16, st)
        MT = kb.sb('r_MT', [64, 4, 64], BF16, st)
        sc = kb.sb('r_sc', [128, 24], F32, st)
        pbuf = kb.sb('r_pbuf', [128, 4, NMEM], BF16, st)
        pT = kb.sb('r_pT', [128, 1024], BF16, st)
        yT = XT[:, 0:4, :, :].rearrange("p a b t -> p (a b) t")
        kb.barrier()

        def flat3(b):
            return ps[:, b:b + 3, :].rearrange("p b n -> p (b n)")

        def hv(t):
            return t.rearrange("p (h n) -> p h n", n=64)

        def bc(t24):
            return t24.unsqueeze(2).to_broadcast([128, 24, 64])

        for ti in range(NT if stage == 2 else 1):
            p = ti % NB
            P = lambda n: '%s%d' % (n, p)
            rows = slice(ti * 128, (ti + 1) * 128)
            tok = slice(3 + ti * 128, 3 + (ti + 1) * 128)
            fr = [('featT', ti)]
            kb.dma('sp', rkv[:], rkv_d[rows, :], [('rkv', ti)], ['rkv'])
            kb.dma('sp', vf[:], vfirst_d[rows, :], [('vfirst', ti)], ['xs'])
            kb.dma('sp', szb[p][:], sz_d[rows, :], [('sz', ti)], [P('szt')])
            kb.dma('sp', xtb[p][:], xres_d[rows, :], [('xres', ti)], [P('xt')])
            r_ = rkv[:, 0:DMIX]
            k_ = rkv[:, DMIX:2 * DMIX]
            v_ = rkv[:, 2 * DMIX:3 * DMIX]
            b, rb = kb.banks(3)
            for nb in range(3):
                cs = slice(nb * 512, (nb + 1) * 512)
                kb.mm(ps[:, b + nb, :], featT[0:64, 4, tok], wa2[0:64, cs], True, False, fr + ['wa2'], [rb[nb]])
                kb.mm(ps[:, b + nb, :], sel[0:3, 0, :], brows[0:3, cs], False, True, ['sel', 'brows'], [rb[nb]])
            kb.act(sg[:], flat3(b), AF.Sigmoid, rb, ['sg'])
            b, rb = kb.banks(3)
            for nb in range(3):
                cs = slice(nb * 512, (nb + 1) * 512)
                kb.mm(ps[:, b + nb, :], featT[64:128, 4, tok], wa2[64:128, cs], True, False, fr + ['wa2'], [rb[nb]])
                kb.mm(ps[:, b + nb, :], sel[0:3, 1, :], brows[0:3, cs], False, True, ['sel', 'brows'], [rb[nb]], ser=True)
            kb.act(av[:], flat3(b), AF.Sigmoid, rb, ['av'])
            b, rb = kb.banks(3)
            for nb in range(3):
                cs = slice(nb * 512, (nb + 1) * 512)
                kb.mm(ps[:, b + nb, :], featT[0:32, 5, tok], vl2[0:32, cs], True, False, fr + ['vl2'], [rb[nb]])
                kb.mm(ps[:, b + nb, :], sel[0:3, 2, :], brows[0:3, cs], False, True, ['sel', 'brows'], [rb[nb]])
            kb.act(bon[:], flat3(b), AF.Sigmoid, rb, ['bon'])
            b, rb = kb.banks(3)
            for nb in range(3):
                cs = slice(nb * 512, (nb + 1) * 512)
                kb.mm(ps[:, b + nb, :], featT[:, 6, tok], gl2[:, cs], True, True, fr + ['gl2'], [rb[nb]])
            kb.cp('act', gg[:], flat3(b), rb, ['gg'])
            kb.tt('pool', tmpf[:], vf[:], v_, ALU.subtract, ['xs', 'rkv'], ['tmpf'])
            kb.tt('dve', tmpf[:], tmpf[:], bon[:], ALU.mult, ['tmpf', 'bon'], ['tmpf'])
            kb.tt('pool', V[:], tmpf[:], v_, ALU.add, ['tmpf', 'rkv'], ['V'])
            kb.tt('dve', kkn[:], k_, reps[:, 0, :], ALU.mult, ['rkv', 'reps'], ['kkn'])
            kb.tt('pool', tmpf[:], kkn[:], kkn[:], ALU.mult, ['kkn', 'tmpf'], ['tmpf'])
            kb.op('dve', lambda e: e.tensor_reduce(out=sm[:, 0:24], in_=hv(tmpf[:]), axis=AX.X, op=ALU.add), ['tmpf'], ['sm'])
            kb.act(sm[:, 0:24], sm[:, 0:24], AF.Sqrt, ['sm'], ['sm'])
            kb.ts('dve', sm[:, 0:24], sm[:, 0:24], 1e-12, None, ALU.max, None, ['sm'], ['sm'])
            kb.op('dve', lambda e: e.reciprocal(out=sm[:, 24:48], in_=sm[:, 0:24]), ['sm'], ['sm'])
            kb.tt('dve', hv(kkn[:]), hv(kkn[:]), bc(sm[:, 24:48]), ALU.mult, ['kkn', 'sm'], ['kkn'])
            kb.stt(tmpf[:], av[:], -1.0, reps[:, 1, :], ALU.add, ALU.mult, ['av', 'reps', 'tmpf'], ['tmpf'])
            kb.stt(kp[:], tmpf[:], 1.0, k_, ALU.add, ALU.mult, ['tmpf', 'rkv'], ['kp'])
            kb.tt('pool', bv[:], kkn[:], av[:], ALU.mult, ['kkn', 'av'], ['bv'])
            kb.tt('pool', tmpf[:], r_, kp[:], ALU.mult, ['rkv', 'kp', 'tmpf'], ['tmpf'])
            kb.tt('dve', tmpf[:], tmpf[:], reps[:, 2, :], ALU.mult, ['tmpf', 'reps'], ['tmpf'])
            kb.op('dve', lambda e: e.tensor_reduce(out=sm[:, 48:72], in_=hv(tmpf[:]), axis=AX.X, op=ALU.add), ['tmpf'], ['sm2'])
            kb.tt('pool', hv(bon[:]), hv(V[:]), bc(sm[:, 48:72]), ALU.mult, ['V', 'sm2', 'bon'], ['bon'])
            bw, rbw = kb.banks(1)
            for h_ in range(24):
                kb.mm(ps[0:64, bw, h_ * 2:h_ * 2 + 2], sg[:, h_ * 64:(h_ + 1) * 64], negcol[:, 0:2], True, True, ['sg', 'negcol'], rbw)
            kb.act(WLT[:], ps[0:64, bw, 0:48].rearrange("p (h two) -> p h two", two=2)[:, :, 0], AF.Exp, rbw, ['WLT'])
            b, rb = kb.banks(3)
            for nb in range(3):
                kb.mm(ps[:, b + nb, :], tsc_incl[:], sg[:, nb * 512:(nb + 1) * 512], True, True, ['tsc_incl', 'sg'], [rb[nb]])
            kb.act(ef[:], flat3(b), AF.Exp, rb, ['ef'])

            def transposes(idx):
                b_, rb_ = kb.banks(2)
                for hp in range(12):
                    kb.tr(psb[:, b_ + hp // 8, (hp % 8) * 128:(hp % 8 + 1) * 128], xs[:, hp * 128:(hp + 1) * 128], identb[:], ['xs', 'identb'], [rb_[hp // 8]])
                kb.cp('act', XT[:, 0:8, idx, :], psb[:, b_, :].rearrange("p (c t) -> p c t", c=8), [rb_[0]], ['XT', 'yT0', 'yT1'])
                kb.cp('act', XT[:, 8:12, idx, :], psb[:, b_ + 1, 0:512].rearrange("p (c t) -> p c t", c=4), [rb_[1]], ['XT', 'yT0', 'yT1'])

            kb.tt('dve', xs[:], r_, ef[:], ALU.mult, ['rkv', 'ef'], ['xs'])
            transposes(1)
            kb.cp('pool', rt[:], xs[:], ['xs'], ['rt'])
            kb.op('dve', lambda e: e.reciprocal(out=ef[:], in_=ef[:]), ['ef'], ['ef'])
            kb.tt('pool', xs[:], bv[:], ef[:], ALU.mult, ['bv', 'ef'], ['xs'])
            transposes(2)
            kb.tt('dve', xs[:], kp[:], ef[:], ALU.mult, ['kp', 'ef'], ['xs'])
            transposes(3)
            b, rb = kb.banks(3)
            for nb in range(3):
                kb.mm(ps[:, b + nb, :], tsc_strict[:], sg[:, nb * 512:(nb + 1) * 512], True, True, ['tsc_strict', 'sg'], [rb[nb]])
            kb.act(ef[:], flat3(b), AF.Exp, rb, ['ef'])
            kb.stt(at[:], kkn[:], -1.0, ef[:], ALU.mult, ALU.mult, ['kkn', 'ef'], ['at'])
            kb.cp('pool', xs[:], at[:], ['at'], ['xs'])
            transposes(0)
            b, rb = kb.banks(3)
            for nb in range(3):
                kb.mm(ps[:, b + nb, :], tsc_rev[:], sg[:, nb * 512:(nb + 1) * 512], True, True, ['tsc_rev', 'sg'], [rb[nb]])
            kb.act(ef[:], flat3(b), AF.Exp, rb, ['ef'])
            kb.tt('dve', bbar[:], bv[:], ef[:], ALU.mult, ['bv', 'ef'], ['kkn'])
            kb.tt('pool', kbar[:], kp[:], ef[:], ALU.mult, ['kp', 'ef'], ['av'])

            for gI in range(6 if stage in (2, 4) else (0 if stage == 5 else 1)):
                hidx = lambda q: 4 * gI + 2 * (q % 2) + q // 2
                hcol = lambda q: slice(hidx(q) * 64, (hidx(q) + 1) * 64)
                bA, rbA = kb.banks(4)
                for q in range(4):
                    hp = 2 * gI + q % 2
                    prt = slice(64 * (q // 2), 64 * (q // 2) + 64)
                    rhsAR = XT[prt, hp, 0:2, :].rearrange("p a t -> p (a t)")
                    co = slice((q % 2) * 256, (q % 2) * 256 + 256)
                    kb.mm(ps[:, bA + (q // 2) * 2, co], XT[prt, hp, 2, :], rhsAR, True, True, ['XT'], [rbA[(q // 2) * 2]])
                    kb.mm(ps[:, bA + (q // 2) * 2 + 1, co], XT[prt, hp, 3, :], rhsAR, True, True, ['XT'], [rbA[(q // 2) * 2 + 1]])
                for pr in range(2):
                    kb.tt('dve', ATab[:, 2 * pr:2 * pr + 2, :, :].rearrange("p q k t -> p (q k t)"), ps[:, bA + 2 * pr, :], mST2[:].rearrange("p q t -> p (q t)"),
                          ALU.mult, [rbA[2 * pr], 'mST2'], ['ATab'])
                    kb.tt('dve', ATak[:, 2 * pr:2 * pr + 2, :, :].rearrange("p q k t -> p (q k t)"), ps[:, bA + 2 * pr + 1, :], mST2[:].rearrange("p q t -> p (q t)"),
                          ALU.mult, [rbA[2 * pr + 1], 'mST2'], ['ATak'])
                bZ, rbZ = kb.banks(2)
                for q in range(4):
                    hp = 2 * gI + q % 2
                    prt = slice(64 * (q // 2), 64 * (q // 2) + 64)
                    kb.mm(ps[:, bZ + q // 2, (q % 2) * 128:(q % 2 + 1) * 128], XT[prt, hp, 0, :], XT[prt, hp, 2, :], True, True, ['XT'], [rbZ[q // 2]])
                for hd in range(2):
                    kb.tt('dve', Pb[0][:, 2 * hd:2 * hd + 2, :].rearrange("p q t -> p (q t)"), ps[:, bZ + hd, 0:256], mTS[:, 0:2, :].rearrange("p q t -> p (q t)"),
                          ALU.mult, [rbZ[hd], 'mTS'], ['P0'])
                kb.tt('pool', TTb[0][:], ATab[:, :, 0, :], identb[:].unsqueeze(1).to_broadcast([128, 4, 128]), ALU.add, ['ATab', 'identb'], ['TT0'])
                cur = 0
                for lev in range(1, 8):
                    Qc = [ATab[:, q, 0, :] for q in range(4)] if lev == 1 else [Qb[cur][:, q, :] for q in range(4)]
                    rQ = 'ATab' if lev == 1 else 'Q%d' % cur
                    Pc = [Pb[cur][:, q, :] for q in range(4)]
                    rP = 'P%d' % cur
                    nxt = 1 - cur
                    if lev >= 2:
                        bX, rbX = kb.banks(1)
                        for q in range(4):
                            kb.mm(ps[:, bX, q * 128:(q + 1) * 128], Pc[q], TTb[cur][:, q, :], True, True, [rP, 'TT%d' % cur], rbX)
                    if lev <= 6:
                        bP, rbP = kb.banks(1)
                        for q in range(4):
                            kb.mm(ps[:, bP, q * 128:(q + 1) * 128], Qc[q], Pc[q], True, True, [rQ, rP], rbP)
                    if lev <= 5:
                        bQ, rbQ = kb.banks(1)
                        for q in range(4):
                            kb.mm(ps[:, bQ, q * 128:(q + 1) * 128], Pc[q], Qc[q], True, True, [rQ, rP], rbQ)
                    if lev >= 2:
                        kb.tt('dve', TTb[nxt][:].rearrange("p q t -> p (q t)"), ps[:, bX, :], TTb[cur][:].rearrange("p q t -> p (q t)"), ALU.add,
                              rbX + ['TT%d' % cur], ['TT%d' % nxt])
                    else:
                        kb.cp('pool', TTb[nxt][:], TTb[cur][:], ['TT%d' % cur], ['TT%d' % nxt])
                    if lev <= 6:
                        kb.cp('act', Pb[nxt][:].rearrange("p q t -> p (q t)"), ps[:, bP, :], rbP, ['P%d' % nxt])
                    if lev <= 5:
                        kb.cp('act', Qb[nxt][:].rearrange("p q t -> p (q t)"), ps[:, bQ, :], rbQ, ['Q%d' % nxt])
                    cur = nxt
                TT = TTb[cur]
                rTT = 'TT%d' % cur
                bX, rbX = kb.banks(1)
                for q in range(4):
                    kb.mm(ps[:, bX, q * 64:(q + 1) * 64], ATak[:, q, 0, :], V[:, hcol(q)], True, True, ['ATak', 'V'], rbX)
                kb.cp('act', X1s[:].rearrange("p q e -> p (q e)"), ps[:, bX, 0:256], rbX, ['X1s'])
                bU, rbU = kb.banks(1)
                for q in range(4):
                    kb.mm(ps[:, bU, q * 128:q * 128 + 64], TT[:, q, :], at[:, hcol(q)], True, True, [rTT, 'at'], rbU)
                    kb.mm(ps[:, bU, q * 128 + 64:(q + 1) * 128], TT[:, q, :], X1s[:, q, :], True, True, [rTT, 'X1s'], rbU)
                kb.cp('act', AU[:].rearrange("p q e -> p (q e)"), ps[:, bU, :], rbU, ['AU'])
                bR, rbR = kb.banks(1)
                for q in range(4):
                    kb.mm(ps[0:64, bR, q * 128:(q + 1) * 128], AU[:, q, 0:64], ATab[:, q, 1, :], True, False, ['AU', 'ATab'], rbR)
                    kb.mm(ps[0:64, bR, q * 128:(q + 1) * 128], rt[:, hcol(q)], identb[:], False, True, ['rt', 'identb'], rbR)
                kb.cp('act', RT[:].rearrange("p q t -> p (q t)"), ps[0:64, bR, :], rbR, ['RT'])
                bM, rbM = kb.banks(1)
                for q in range(4):
                    kb.mm(ps[0:64, bM, q * 64:(q + 1) * 64], AU[:, q, 0:64], bbar[:, hcol(q)], True, True, ['AU', 'kkn'], rbM)
                kb.cp('act', MT[:].rearrange("p a e -> p (a e)"), ps[0:64, bM, 0:256], rbM, ['MT'])
                bY, rbY = kb.banks(1)
                for q in range(4):
                    o = ps[:, bY, q * 64:(q + 1) * 64]
                    kb.mm(o, ATab[:, q, 1, :], AU[:, q, 64:128], True, False, ['ATab', 'AU'], rbY)
                    kb.mm(o, ATak[:, q, 1, :], V[:, hcol(q)], False, False, ['ATak', 'V'], rbY)
                    kb.mm(o, RT[:, q, :], STb[:, hidx(q), :], False, True, ['RT', 'STb'], rbY)
                for q in range(4):
                    kb.cp('act', tmpf[:, hcol(q)], ps[:, bY, q * 64:(q + 1) * 64], rbY, ['yo'])
                bS, rbS = kb.banks(1)
                for q in range(4):
                    o = ps[0:64, bS, q * 64:(q + 1) * 64]
                    kb.mm(o, MT[:, q, :], STb[:, hidx(q), :], True, False, ['MT', 'STb'], rbS)
                    kb.mm(o, bbar[:, hcol(q)], AU[:, q, 64:128], False, False, ['kkn', 'AU'], rbS)
                    kb.mm(o, kbar[:, hcol(q)], V[:, hcol(q)], False, True, ['av', 'V'], rbS)
                for q in range(4):
                    h_ = hidx(q)
                    kb.stt(STf[:, h_, :], STf[:, h_, :], WLT[:, h_:h_ + 1], ps[0:64, bS, q * 64:(q + 1) * 64], ALU.mult, ALU.add, rbS + ['STf', 'WLT'], ['STf'])
                kb.cp('act', STb[:, 4 * gI:4 * gI + 4, :], STf[:, 4 * gI:4 * gI + 4, :], ['STf'], ['STb'])

            yo = tmpf
            kb.op('dve', lambda e: e.tensor_reduce(out=sm[:, 72:96], in_=hv(yo[:]), axis=AX.X, op=ALU.add), ['yo', 'tmpf'], ['sm3'])
            kb.ts('dve', sm[:, 72:96], sm[:, 72:96], 1.0 / 64, None, ALU.mult, None, ['sm3'], ['sm3'])
            kb.tt('dve', hv(yo[:]), hv(yo[:]), bc(sm[:, 72:96]), ALU.subtract, ['yo', 'sm3'], ['yo'])
            kb.tt('pool', ef[:], yo[:], yo[:], ALU.mult, ['yo', 'ef'], ['ef'])
            kb.op('dve', lambda e: e.tensor_reduce(out=sm[:, 96:120], in_=hv(ef[:]), axis=AX.X, op=ALU.add), ['ef'], ['sm4'])
            kb.act(sm[:, 96:120], sm[:, 96:120], AF.Sqrt, ['sm4', 'epsc'], ['sm4'], scale=1.0 / 64, bias=epsc[:, 2:3])
            kb.op('dve', lambda e: e.reciprocal(out=sm[:, 96:120], in_=sm[:, 96:120]), ['sm4'], ['sm4'])
            kb.tt('dve', hv(yo[:]), hv(yo[:]), bc(sm[:, 96:120]), ALU.mult, ['yo', 'sm4'], ['yo'])
            kb.tt('pool', yo[:], yo[:], reps[:, 3, :], ALU.mult, ['yo', 'reps'], ['yo'])
            kb.tt('dve', yo[:], yo[:], reps[:, 4, :], ALU.add, ['yo', 'reps'], ['yo'])
            kb.tt('pool', yo[:], yo[:], bon[:], ALU.add, ['yo', 'bon'], ['yo'])
            kb.tt('dve', yo[:], yo[:], gg[:], ALU.mult, ['yo', 'gg'], ['yo'])
            kb.tt('pool', szb[p][:, 0:DMIX], yo[:], szb[p][:, 0:DMIX], ALU.mult, ['yo', P('szt')], [P('szt'), 'tmpf'])
            attn_and_out(ti, 0, szb[p], P('szt'), xtb[p], P('xt'), (sc, pbuf, pT, yT), out_d, final=True, yx=['XT'])
        kb.barrier()

    kb.barrier()
    return kb


def _bd(w):
    n = w.shape[0]
    bd = np.zeros((12, 128, 128), np.float32)
    bdT = np.zeros((12, 128, 128), np.float32)
    for c in range(12):
        for j in range(32):
            blk = w[c * 32 + j]
            bd[c, 4 * j:4 * j + 4, 4 * j:4 * j + 4] = blk
            bdT[c, 4 * j:4 * j + 4, 4 * j:4 * j + 4] = blk.T
    return (np.ascontiguousarray(bd.transpose(1, 0, 2)), np.ascontiguousarray(bdT.transpose(1, 0, 2)))


_NC_CACHE = {}


def kernel(stage=2, **inp):
    f = lambda a: np.ascontiguousarray(np.asarray(a, dtype=np.float32))
    if stage not in _NC_CACHE:
        _NC_CACHE[stage] = build(stage)
    kb = _NC_CACHE[stage]
    shared = {
        'norm_g': f(inp['norm_g']), 'mem_norm_g': f(inp['mem_norm_g']), 'final_g': f(inp['final_g']).reshape(1, D),
        'kv0': f(inp['mem_kv_w'][0]), 'kv1': f(inp['mem_kv_w'][1]),
        'wout0': f(inp['w_out'][0]), 'wout1': f(inp['w_out'][1]),
        'ml_w_in': f(inp['ml_w_in'][0]),
        'cwT': f(np.asarray(inp['ml_conv_w'][0]).T.reshape(12, 128, 4).transpose(1, 0, 2)),
        'conv_b': f(inp['ml_conv_b'][0]).reshape(1, DMIX),
        'wg': f(np.asarray(inp['ml_w_gate'][0]).reshape(36, 128, 8).transpose(1, 0, 2)),
        'b_gate': f(inp['ml_b_gate'][0]).reshape(1, 8),
        'mhn_g': f(inp['ml_mhn_g'][0]).reshape(1, DMIX), 'skipT': f(np.asarray(inp['ml_skip'][0]).reshape(12, 128).T),
    }
    shared.update({
        'rw_w_in': f(inp['rw_w_in'][0]), 'rw_mu': f(inp['rw_mu'][0]).reshape(1, 4896),
        'wa_lora2': f(np.concatenate([np.asarray(inp['rw_w_lora2'][0]), np.asarray(inp['rw_a_lora2'][0])], axis=0)),
        'v_lora2': f(inp['rw_v_lora2'][0]), 'g_lora2': f(inp['rw_g_lora2'][0]),
        'brows': f(np.stack([np.asarray(inp['rw_w0'][0]), np.asarray(inp['rw_a0'][0]), np.asarray(inp['rw_v0'][0])], axis=0)),
        'reps': f(np.stack([np.asarray(inp['rw_k_k'][0]), np.asarray(inp['rw_k_a'][0]), np.asarray(inp['rw_r_k'][0]).reshape(DMIX),
                            np.asarray(inp['rw_lnx_g'][0]), np.asarray(inp['rw_lnx_b'][0])], axis=0)),
    })
    for n, k in [('bdq', 'ml_wq'), ('bdk', 'ml_wk'), ('bdv', 'ml_wv')]:
        bd, bdT = _bd(np.asarray(inp[k][0], dtype=np.float32))
        shared[n] = bd
        shared[n + 'T'] = bdT
    x = np.asarray(inp['x'], dtype=np.float32)
    mem = np.asarray(inp['mem'], dtype=np.float32)
    in_maps = []
    for c in range(8):
        m = dict(shared)
        m['x'] = np.ascontiguousarray(x[c])
        m['mem'] = np.ascontiguousarray(mem[c])
        in_maps.append(m)
    res = run_bass_kernel_spmd(kb.nc, in_maps, core_ids=list(range(8)))
    return np.stack([np.asarray(r['out'], dtype=np.float32) for r in res.results], axis=0)
```

```python
import numpy as np
from contextlib import ExitStack
import concourse.bass as bass
import concourse.mybir as mybir
from concourse.bass_utils import run_bass_kernel_spmd

F32 = mybir.dt.float32
BF16 = mybir.dt.bfloat16
I32 = mybir.dt.int32
AF = mybir.ActivationFunctionType
ALU = mybir.AluOpType
AX = mybir.AxisListType

S = 2048
D = 1024
NT = S // 128
DMIX = 1536
DX = 512
NMEM = 256


class KB:
    def __init__(self):
        self.nc = bass.Bass("TRN2", target_bir_lowering=False)
        nc = self.nc
        self.es = ExitStack()
        self.engs = {'pe': nc.tensor, 'act': nc.scalar, 'dve': nc.vector, 'pool': nc.gpsimd, 'sp': nc.sync}
        self.sems = {}
        for e in ['pe', 'act', 'dve', 'pool']:
            self.sems[e] = self.es.enter_context(nc.semaphore('c_' + e))
        self.cnt = {e: 0 for e in ['pe', 'act', 'dve', 'pool']}
        self.waited = {e: {} for e in self.engs}
        self.dq = {}
        for q, n in [('sp', 20), ('pool', 12), ('act', 6)]:
            for i in range(n):
                self.sems[(q, i)] = self.es.enter_context(nc.semaphore('d_%s%d' % (q, i)))
            self.dq[q] = [n, 0]
        self.dval = {}
        self.res = {}
        self.nbank = 0

    def sb(self, name, shape, dtype, stack=None):
        return (stack or self.es).enter_context(self.nc.sbuf_tensor('s_' + name, shape, dtype))

    def din(self, name, shape, dtype=F32):
        return self.nc.dram_tensor(name, shape, dtype, kind="ExternalInput").ap()

    def dout(self, name, shape, dtype=F32):
        return self.nc.dram_tensor(name, shape, dtype, kind="ExternalOutput").ap()

    def dint(self, name, shape, dtype=F32):
        return self.nc.dram_tensor(name, shape, dtype, kind="Internal").ap()

    def _wait(self, eng, key, val):
        if self.waited[eng].get(key, 0) >= val:
            return
        self.engs[eng].wait_ge(self.sems[key], val)
        self.waited[eng][key] = val

    def _deps(self, R, W):
        deps = {}

        def add(m):
            if m is None:
                return
            k, v = m
            if deps.get(k, 0) < v:
                deps[k] = v
        for r in R:
            st = self.res.get(r)
            if st:
                add(st[0])
        for w in W:
            st = self.res.get(w)
            if st:
                add(st[0])
                for k, v in st[1].items():
                    add((k, v))
        return deps

    def _mark(self, mark, R, W):
        for r in R:
            st = self.res.setdefault(r, [None, {}])
            if st[1].get(mark[0], 0) < mark[1]:
                st[1][mark[0]] = mark[1]
        for w in W:
            self.res[w] = [mark, {}]

    def op(self, eng, fn, R=(), W=()):
        deps = self._deps(R, W)
        for k, v in deps.items():
            if eng == 'pe' and k == 'pe':
                continue
            self._wait(eng, k, v)
        inst = fn(self.engs[eng])
        self.cnt[eng] += 1
        inst.then_inc(self.sems[eng], 1)
        self._mark((eng, self.cnt[eng]), R, W)
        return inst

    def dma(self, q, out, in_, R=(), W=()):
        n, i = self.dq[q]
        self.dq[q][1] = i + 1
        key = (q, i % n)
        gen = i // n
        if gen > 0:
            self._wait(q, key, 16 * gen)
        deps = self._deps(R, W)
        for k, v in deps.items():
            self._wait(q, k, v)
        self.engs[q].dma_start(out=out, in_=in_).then_inc(self.sems[key], 16)
        self.dval[key] = 16 * (gen + 1)
        self._mark((key, 16 * (gen + 1)), R, W)

    def barrier(self):
        for e in self.engs:
            for k in ['pe', 'act', 'dve', 'pool']:
                if self.cnt[k] > 0:
                    self._wait(e, k, self.cnt[k])
            for key, v in self.dval.items():
                self._wait(e, key, v)

    def banks(self, n):
        if self.nbank + n > 8:
            self.nbank = 0
        b = self.nbank
        self.nbank += n
        return b, [('ps', j) for j in range(b, b + n)]

    def _pe_mode(self, lhsT, ser):
        r = lambda n: 32 if n <= 32 else (64 if n <= 64 else 128)
        shp = lhsT.shape
        k = int(shp[0])
        m = 1
        for d in shp[1:]:
            m *= int(d)
        mode = (r(k), r(m), int(lhsT.offset) if False else 0)
        if (ser or mode != getattr(self, 'pe_mode', None)) and self.cnt['pe'] > 0:
            self._wait('pe', 'pe', self.cnt['pe'])
        self.pe_mode = mode

    def mm(self, out, lhsT, rhs, start, stop, R, W, ser=False):
        self._pe_mode(lhsT, ser)
        return self.op('pe', lambda e: e.matmul(out, lhsT, rhs, start=start, stop=stop), R, W)

    def tr(self, out, in_, ident, R, W):
        self._pe_mode(in_, False)
        return self.op('pe', lambda e: e.transpose(out, in_, ident), R, W)

    def act(self, out, in_, func, R, W, eng='act', **kw):
        return self.op('act', lambda e: e.activation(out=out, in_=in_, func=func, **kw), R, W)

    def tt(self, eng, out, in0, in1, op, R, W):
        return self.op(eng, lambda e: e.tensor_tensor(out=out, in0=in0, in1=in1, op=op), R, W)

    def ts(self, eng, out, in0, s1, s2, op0, op1, R, W, **kw):
        if op1 is None:
            return self.op(eng, lambda e: e.tensor_scalar(out=out, in0=in0, scalar1=s1, scalar2=None, op0=op0, **kw), R, W)
        return self.op(eng, lambda e: e.tensor_scalar(out=out, in0=in0, scalar1=s1, scalar2=s2, op0=op0, op1=op1, **kw), R, W)

    def stt(self, out, in0, scalar, in1, op0, op1, R, W):
        return self.op('dve', lambda e: e.scalar_tensor_tensor(out=out, in0=in0, scalar=scalar, in1=in1, op0=op0, op1=op1), R, W)

    def cp(self, eng, out, in_, R, W):
        if eng == 'act':
            return self.op('act', lambda e: e.copy(out=out, in_=in_), R, W)
        return self.op(eng, lambda e: e.tensor_copy(out=out, in_=in_), R, W)

    def memset(self, eng, ap, val, W):
        return self.op(eng, lambda e: e.memset(ap, val), (), W)


def build(stage=2):
    kb = KB()
    nc = kb.nc
    x_d = kb.din('x', [S, D])
    mem_d = kb.din('mem', [NMEM, D])
    norm_g_d = kb.din('norm_g', [2, D])
    memg_d = kb.din('mem_norm_g', [2, D])
    fing_d = kb.din('final_g', [1, D])
    kv_d = [kb.din('kv0', [D, D]), kb.din('kv1', [D, D])]
    wout_d = [kb.din('wout0', [2048, D]), kb.din('wout1', [2048, D])]
    mlw_d = kb.din('ml_w_in', [D, 4096])
    bd_d = {n: kb.din(n, [128, 12, 128]) for n in ['bdq', 'bdk', 'bdv', 'bdqT', 'bdkT', 'bdvT']}
    cwT_d = kb.din('cwT', [128, 12, 4])
    convb_d = kb.din('conv_b', [1, DMIX])
    wg_d = kb.din('wg', [128, 36, 8])
    bgate_d = kb.din('b_gate', [1, 8])
    mhng_d = kb.din('mhn_g', [1, DMIX])
    skipT_d = kb.din('skipT', [128, 12])
    out_d = kb.dout('out', [S, D])
    xres_d = kb.dint('xres', [S, D])
    sz_d = kb.dint('sz', [S, 2048], BF16)
    vfirst_d = kb.dint('vfirst', [S, DMIX], BF16)
    uT_d = kb.dint('uT', [12, 128, 3 + S], BF16)
    rww_d = kb.din('rw_w_in', [D, 7456])
    mu_d = kb.din('rw_mu', [1, 4896])
    wa2_d = kb.din('wa_lora2', [128, DMIX])
    vl2_d = kb.din('v_lora2', [32, DMIX])
    gl2_d = kb.din('g_lora2', [128, DMIX])
    brow_d = kb.din('brows', [3, DMIX])
    reps_d = kb.din('reps', [5, DMIX])
    rkv_d = kb.dint('rkv', [S, 4608], BF16)

    ps = kb.es.enter_context(nc.psum_tensor('ps', [128, 8, 512], F32))
    psb = ps.bitcast(BF16)
    identf = kb.sb('identf', [128, 128], F32)
    identb = kb.sb('identb', [128, 128], BF16)
    tri_incl = kb.sb('tri_incl', [128, 128], F32)
    onesf = kb.sb('onesf', [128, 128], F32)
    onesb = kb.sb('onesb', [1, 128], BF16)
    epsc = kb.sb('epsc', [128, 4], F32)
    featT = kb.sb('featT', [128, 7, 3 + S], BF16)
    wout = kb.sb('wout', [128, 16, D], BF16)
    kmT = kb.sb('kmT', [128, 4, NMEM], BF16)
    vm = kb.sb('vm', [128, 2, DX], BF16)
    grep = kb.sb('grep', [128, D], F32)

    iot = kb.sb('iot', [128, 128], I32)
    if True:
        kb.op('pool', lambda e: e.iota(iot[:], pattern=[[1, 128]], base=0, channel_multiplier=-1), (), ['iot'])
        kb.op('dve', lambda e: e.tensor_single_scalar(out=identf[:], in_=iot[:], scalar=0, op=ALU.is_equal), ['iot'], ['identf'])
        kb.op('dve', lambda e: e.tensor_single_scalar(out=identb[:], in_=iot[:], scalar=0, op=ALU.is_equal), ['iot'], ['identb'])
        kb.op('dve', lambda e: e.tensor_single_scalar(out=tri_incl[:], in_=iot[:], scalar=0, op=ALU.is_ge), ['iot'], ['tri_incl'])
        kb.memset('dve', onesf[:], 1.0, ['onesf'])
        kb.memset('dve', onesb[:], 1.0, ['onesb'])
        kb.memset('dve', epsc[:, 0:1], 1e-6, ['epsc'])
        kb.memset('dve', epsc[:, 1:2], 1e-5, ['epsc'])
        kb.memset('dve', epsc[:, 2:3], 64e-5, ['epsc'])
        kb.memset('dve', epsc[:, 3:4], 1.0, ['epsc'])
        zt = kb.sb('zt', [128, 12, 4], BF16)
        kb.memset('dve', zt[:], 0.0, ['zt'])
        kb.dma('pool', uT_d[:, :, 0:3].rearrange("c p t -> p c t"), zt[:, :, 0:3], ['zt'], ['uT_z'])
        kb.barrier()

    def phase_A(src, ntiles, grow, hT, col0, st, tag):
        kb.dma('sp', grep[:], grow.to_broadcast([128, D]), (), ['grep'])
        xb = [kb.sb('%s_x%d' % (tag, i), [128, D], F32, st) for i in range(2)]
        junk = kb.sb(tag + '_junk', [128, D], BF16, st)
        hs = [kb.sb('%s_hs%d' % (tag, i), [128, D], BF16, st) for i in range(2)]
        ss = kb.sb(tag + '_ss', [128, 8], F32, st)
        if col0 > 0:
            kb.memset('dve', hT[:, :, 0:col0], 0.0, ['hT_z'])
        for i in range(ntiles):
            xt = xb[i % 2]
            rx = 'A_x%d' % (i % 2)
            rss = ('A_ss', i % 2)
            kb.dma('sp', xt[:], src[i * 128:(i + 1) * 128, :], (), [rx])
            c = (i % 2) * 4
            kb.act(junk[:], xt[:], AF.Square, [rx], ['A_junk', rss], accum_out=ss[:, c:c + 1])
            kb.act(ss[:, c + 1:c + 2], ss[:, c:c + 1], AF.Sqrt, [rss, 'epsc'], [rss], scale=1.0 / D, bias=epsc[:, 0:1])
            kb.op('dve', lambda e: e.reciprocal(out=ss[:, c + 2:c + 3], in_=ss[:, c + 1:c + 2]), [rss], [rss])
            h = hs[i % 2]
            rh = 'A_hs%d' % (i % 2)
            kb.stt(h[:], xt[:], ss[:, c + 2:c + 3], grep[:], ALU.mult, ALU.mult, [rx, rss, 'grep'], [rh])
            b, rb = kb.banks(1)
            for k in range(8):
                kb.tr(psb[:, b, k * 128:(k + 1) * 128], h[:, k * 128:(k + 1) * 128], identb[:], [rh, 'identb'], rb)
            kb.cp('act' if i % 2 else 'dve', hT[:, :, col0 + i * 128: col0 + (i + 1) * 128],
                  psb[:, b, :].rearrange("p (k t) -> p k t", k=8), rb, [('hT', i)])

    def load_w_block(Wd, c0, ncols, wb, rw):
        kb.dma('pool', wb[:, :, 0:ncols], Wd[:, c0:c0 + ncols].rearrange("(k p) c -> p k c", p=128), (), [rw])

    def proj_T(hT, col0, ti, hres, wb, rw, ncols, wb2=None, rw2=None):
        b, rb = kb.banks(1)
        n = 16 if wb2 is not None else 8
        j = 0
        for k in range(8):
            kb.mm(ps[:, b, 0:ncols], hT[:, k, col0 + ti * 128: col0 + (ti + 1) * 128], wb[:, k, 0:ncols], j == 0, j == n - 1, list(hres) + [rw], rb)
            j += 1
        if wb2 is not None:
            for k in range(8):
                kb.mm(ps[:, b, 0:ncols], hT[:, k, col0 - 1 + ti * 128: col0 - 1 + (ti + 1) * 128], wb2[:, k, 0:ncols], False, j == n - 1, list(hres) + [rw2], rb)
                j += 1
        return b, rb

    def proj_F(hT, col0, tb, ntok, hres, wb, rw, m0, M, wb2=None, rw2=None):
        b, rb = kb.banks(1)
        n = 16 if wb2 is not None else 8
        j = 0
        for k in range(8):
            kb.mm(ps[0:M, b, 0:ntok], wb[:, k, m0:m0 + M], hT[:, k, col0 + tb: col0 + tb + ntok], j == 0, j == n - 1, list(hres) + [rw], rb)
            j += 1
        if wb2 is not None:
            for k in range(8):
                kb.mm(ps[0:M, b, 0:ntok], wb2[:, k, m0:m0 + M], hT[:, k, col0 - 1 + tb: col0 - 1 + tb + ntok], False, j == n - 1, list(hres) + [rw2], rb)
                j += 1
        return b, rb

    def mem_kv(layer, st):
        memT = kb.sb('memT%d' % layer, [128, 8, NMEM], BF16, st)
        phase_A(mem_d, 2, memg_d[layer:layer + 1, :], memT, 0, st, 'Am%d' % layer)
        wbk = [kb.sb('kvw%d_%d' % (layer, i), [128, 8, 512], BF16, st) for i in range(2)]
        for i in range(2):
            load_w_block(kv_d[layer], i * 512, 512, wbk[i], 'kvw%d' % i)
        hres = [('hT', 0), ('hT', 1)]
        for h in range(4):
            b, rb = kb.banks(1)
            for k in range(8):
                kb.mm(ps[:, b, 0:NMEM], wbk[0][:, k, h * 128:(h + 1) * 128], memT[:, k, :], k == 0, k == 7, hres + ['kvw0'], rb)
            kb.cp('act', kmT[:, h, :], ps[:, b, 0:NMEM], rb, ['kmT'])
        for mt in range(2):
            b, rb = kb.banks(1)
            for k in range(8):
                kb.mm(ps[:, b, 0:512], memT[:, k, mt * 128:(mt + 1) * 128], wbk[1][:, k, :], k == 0, k == 7, hres + ['kvw1'], rb)
            kb.cp('act', vm[:, mt, :], ps[:, b, 0:512], rb, ['vm'])

    def attn_and_out(ti, qcol, y, ry, xt, rx, st_bufs, dst_d, final=False, yx=()):
        sc, pbuf, pT, yT = st_bufs
        b, rb = kb.banks(2)
        for h in range(4):
            kb.mm(ps[:, b + h // 2, (h % 2) * 256:(h % 2) * 256 + 256], featT[:, qcol + h, 3 + ti * 128: 3 + (ti + 1) * 128], kmT[:, h, :], True, True,
                  [('featT', ti), 'kmT'], [rb[h // 2]])
        kb.op('dve', lambda e: e.tensor_reduce(out=sc[:, 0:4], in_=ps[:, b:b + 2, :].rearrange("p b (h m) -> p (b h) m", h=2), axis=AX.X, op=ALU.max), rb, ['at_sc'])
        kb.ts('dve', sc[:, 4:8], sc[:, 0:4], -(128.0 ** -0.5), None, ALU.mult, None, ['at_sc'], ['at_sc'])
        for h in range(4):
            kb.act(pbuf[:, h, :], ps[:, b + h // 2, (h % 2) * 256:(h % 2) * 256 + 256], AF.Exp, [rb[h // 2], 'at_sc'], ['at_p', ('at_sum', h)],
                   scale=128.0 ** -0.5, bias=sc[:, 4 + h:5 + h], accum_out=sc[:, 8 + h:9 + h])
        kb.op('dve', lambda e: e.reciprocal(out=sc[:, 12:16], in_=sc[:, 8:12]), [('at_sum', h) for h in range(4)], ['at_rinv'])
        b2, rb2 = kb.banks(1)
        for h in range(4):
            for mc in range(2):
                kb.tr(psb[:, b2, (h * 2 + mc) * 128:(h * 2 + mc + 1) * 128], pbuf[:, h, mc * 128:(mc + 1) * 128], identb[:], ['at_p', 'identb'], rb2)
        kb.cp('act', pT[:], psb[:, b2, :], rb2, ['at_pT'])
        b3, rb3 = kb.banks(1)
        for h in range(4):
            for mc in range(2):
                kb.mm(ps[:, b3, h * 128:(h + 1) * 128], pT[:, (h * 2 + mc) * 128:(h * 2 + mc + 1) * 128], vm[:, mc, h * 128:(h + 1) * 128], mc == 0, mc == 1, ['at_pT', 'vm'], rb3)
        for h in range(4):
            kb.stt(y[:, DMIX + h * 128: DMIX + (h + 1) * 128], ps[:, b3, h * 128:(h + 1) * 128], sc[:, 12 + h:13 + h], y[:, DMIX + h * 128: DMIX + (h + 1) * 128],
                   ALU.mult, ALU.mult, rb3 + ['at_rinv', ry], [ry])
        b4, rb4 = kb.banks(2)
        for c in range(16):
            kb.tr(psb[:, b4 + c // 8, (c % 8) * 128:(c % 8 + 1) * 128], y[:, c * 128:(c + 1) * 128], identb[:], [ry, 'identb'], [rb4[c // 8]])
        kb.cp('act', yT[:, 0:8, :], psb[:, b4, :].rearrange("p (c t) -> p c t", c=8), [rb4[0]], ['yT0'] + list(yx))
        kb.cp('dve', yT[:, 8:16, :], psb[:, b4 + 1, :].rearrange("p (c t) -> p c t", c=8), [rb4[1]], ['yT1'] + list(yx))
        b5, rb5 = kb.banks(2)
        for nb in range(2):
            for c in range(16):
                kb.mm(ps[:, b5 + nb, :], yT[:, c, :], wout[:, c, nb * 512:(nb + 1) * 512], c == 0, c == 15, ['yT0', 'yT1', 'wout'], [rb5[nb]])
        kb.tt('dve', xt[:], ps[:, b5:b5 + 2, :].rearrange("p b n -> p (b n)"), xt[:], ALU.add, rb5 + [rx], [rx])
        if final:
            kb.act(pT[:], xt[:], AF.Square, [rx], ['at_pT', 'fin_ss'], accum_out=sc[:, 16:17])
            kb.act(sc[:, 17:18], sc[:, 16:17], AF.Sqrt, ['fin_ss', 'epsc'], ['fin_ss'], scale=1.0 / D, bias=epsc[:, 0:1])
            kb.op('dve', lambda e: e.reciprocal(out=sc[:, 18:19], in_=sc[:, 17:18]), ['fin_ss'], ['fin_ss'])
            kb.stt(xt[:], xt[:], sc[:, 18:19], grep[:], ALU.mult, ALU.mult, [rx, 'fin_ss', 'grep'], [rx])
        kb.dma('pool', dst_d[ti * 128:(ti + 1) * 128, :], xt[:], [rx], [('xres', ti)])

    with ExitStack() as st:
        mem_kv(0, st)
        kb.barrier()
    with ExitStack() as st:
        hT = kb.sb('hT0', [128, 8, S + 1], BF16, st)
        phase_A(x_d, NT, norm_g_d[0:1, :], hT, 1, st, 'A0')
        hres_all = [('hT', i) for i in range(NT)]
        for c4 in range(4):
            kb.dma('pool', wout[:, c4 * 4:(c4 + 1) * 4, :], wout_d[0][c4 * 512:(c4 + 1) * 512, :].rearrange("(c p) n -> p c n", p=128), (), ['wout'])
        wbs = [kb.sb('wb%d' % i, [128, 8, 512], BF16, st) for i in range(2)]
        stg = [kb.sb('stg%d' % i, [128, 512], BF16, st) for i in range(3)]
        nstg = 0
        for blk in range(8):
            wb = wbs[blk % 2]
            rw = 'wb%d' % (blk % 2)
            load_w_block(mlw_d, blk * 512, 512, wb, rw)
            if blk < 4:
                for m in range(4):
                    ch = blk * 4 + m
                    for tb in range(4):
                        b, rb = proj_F(hT, 1, tb * 512, 512, hres_all, wb, rw, m * 128, 128)
                        if blk < 3:
                            sg = stg[nstg % 3]
                            rs = 'stg%d' % (nstg % 3)
                            nstg += 1
                            kb.cp('act' if tb % 2 else 'dve', sg[:], ps[:, b, :], rb, [rs])
                            kb.dma('pool', uT_d[ch, :, 3 + tb * 512: 3 + (tb + 1) * 512], sg[:], [rs], [('uT', tb * 4 + q) for q in range(4)])
                        else:
                            kb.cp('act' if tb % 2 else 'dve', featT[:, m, 3 + tb * 512: 3 + (tb + 1) * 512], ps[:, b, :], rb,
                                  [('featT', tb * 4 + q) for q in range(4)])
            else:
                for ti in range(NT):
                    b, rb = proj_T(hT, 1, ti, [('hT', ti)], wb, rw, 512)
                    sg = stg[nstg % 3]
                    rs = 'stg%d' % (nstg % 3)
                    nstg += 1
                    kb.act(sg[:], ps[:, b, :], AF.Silu, rb, [rs])
                    kb.dma('pool', sz_d[ti * 128:(ti + 1) * 128, (blk - 4) * 512:(blk - 3) * 512], sg[:], [rs], [('sz', ti)])
        kb.barrier()

    with ExitStack() as st:
        Cf = kb.sb('Cf', [128, 12, 385], F32, st)
        Cb = kb.sb('Cb', [128, 12, 385], BF16, st)
        BD = {n: kb.sb('s_' + n, [128, 12, 128], BF16, st) for n in ['bdq', 'bdk', 'bdv']}
        convD = kb.sb('convD', [128, 4, 12, 128], BF16, st)
        diagS = kb.sb('diagS', [128, 12, 128], BF16, st)
        cwT = kb.sb('cwT', [128, 12, 4], F32, st)
        skT = kb.sb('skT', [128, 12], F32, st)
        convb = kb.sb('convb', [1, DMIX], BF16, st)
        Gqk = kb.sb('Gqk', [128, 12, 8], BF16, st)
        Gv = kb.sb('Gv', [128, 12, 8], BF16, st)
        bgate = kb.sb('bgate', [1, 8], BF16, st)
        mhn_rep = kb.sb('mhn_rep', [128, DMIX], BF16, st)
        kb.memset('dve', Cf[:], 0.0, ['Cf'])
        kb.memset('dve', Cb[:], 0.0, ['Cb'])
        for n in ['bdq', 'bdk', 'bdv']:
            kb.dma('pool', BD[n][:], bd_d[n], (), [n])
        kb.dma('sp', cwT[:], cwT_d, (), ['cwT'])
        kb.dma('sp', skT[:], skipT_d, (), ['skT'])
        kb.dma('pool', convb[:], convb_d, (), ['convb'])
        kb.dma('pool', bgate[:], bgate_d, (), ['bgate'])
        kb.dma('pool', mhn_rep[:], mhng_d.to_broadcast([128, DMIX]), (), ['mhn_rep'])
        for kk in range(4):
            for c in range(12):
                kb.ts('dve', convD[:, kk, c, :], identf[:], cwT[:, c, kk:kk + 1], None, ALU.mult, None, ['identf', 'cwT'], ['convD'])
        for c in range(12):
            kb.ts('dve', diagS[:, c, :], identf[:], skT[:, c:c + 1], None, ALU.mult, None, ['identf', 'skT'], ['diagS'])
        with ExitStack() as st2:
            bdT = [kb.sb('bdT%d' % i, [128, 12, 128], F32, st2) for i in range(3)]
            wgf = kb.sb('wgf', [128, 36, 8], F32, st2)
            kb.dma('sp', wgf[:], wg_d, (), ['wgf'])
            for gi, n in enumerate(['bdqT', 'bdkT', 'bdvT']):
                kb.dma('sp', bdT[gi][:], bd_d[n], (), ['bdT%d' % gi])
            b, rb = kb.banks(1)
            for c in range(12):
                kb.mm(ps[:, b, c * 8:c * 8 + 8], bdT[0][:, c, :], wgf[:, c, :], True, False, ['bdT0', 'wgf'], rb)
                kb.mm(ps[:, b, c * 8:c * 8 + 8], bdT[1][:, c, :], wgf[:, 12 + c, :], False, True, ['bdT1', 'wgf'], rb)
            for c in range(12):
                kb.mm(ps[:, b, 96 + c * 8:96 + c * 8 + 8], bdT[2][:, c, :], wgf[:, 24 + c, :], True, True, ['bdT2', 'wgf'], rb)
            kb.cp('dve', Gqk[:], ps[:, b, 0:96].rearrange("p (c g) -> p c g", c=12), rb, ['Gqk'])
            kb.cp('dve', Gv[:], ps[:, b, 96:192].rearrange("p (c g) -> p c g", c=12), rb, ['Gv'])
            kb.barrier()

        NB = 2
        uTb = [kb.sb('uTt%d' % i, [128, 12, 131], BF16, st) for i in range(NB)]
        szb = [kb.sb('szt%d' % i, [128, 2048], BF16, st) for i in range(NB)]
        xtb = [kb.sb('xt%d' % i, [128, D], F32, st) for i in range(NB)]
        xcT = kb.sb('xcT', [128, 12, 128], BF16, st)
        qT = kb.sb('qT', [128, 12, 128], BF16, st)
        kT = kb.sb('kT', [128, 12, 128], BF16, st)
        kw = kb.sb('kw', [128, 4, 384], BF16, st)
        vx = kb.sb('vx', [128, 4, 385], BF16, st)
        PT = kb.sb('PT', [128, 4, 128], BF16, st)
        gs = [kb.sb('gs%d' % i, [128, 64], F32, st) for i in range(NB)]
        hh = kb.sb('hh', [128, DMIX], F32, st)
        bst = kb.sb('bst', [128, 4, 6], F32, st)
        sc = kb.sb('sc', [128, 24], F32, st)
        pbuf = kb.sb('pbuf', [128, 4, NMEM], BF16, st)
        pT = kb.sb('pT', [128, 1024], BF16, st)
        yT = kb.sb('yT', [128, 16, 128], BF16, st)
        kb.memset('dve', vx[:, :, 384:385], 1.0, ['vx1'])

        for ti in range(NT):
            p = ti % NB
            P = lambda n: '%s%d' % (n, p)
            ut = uTb[p]
            ru = P('uTt')
            kb.dma('sp', ut[:], uT_d[:, :, ti * 128: ti * 128 + 131].rearrange("c p t -> p c t"),
                   [('uT', ti), ('uT', max(ti - 1, 0)), 'uT_z'], [ru])
            kb.dma('sp', szb[p][:], sz_d[ti * 128:(ti + 1) * 128, :], [('sz', ti)], [P('szt')])
            kb.dma('sp', xtb[p][:], x_d[ti * 128:(ti + 1) * 128, :], (), [P('xt')])
            b, rb = kb.banks(3)
            for c in range(12):
                o = ps[:, b + c // 4, (c % 4) * 128:(c % 4 + 1) * 128]
                for kk in range(4):
                    kb.mm(o, convD[:, kk, c, :], ut[:, c, kk: kk + 128], kk == 0, False, [ru, 'convD'], [rb[c // 4]])
                kb.mm(o, convb[0:1, c * 128:(c + 1) * 128], onesb[0:1, :], False, True, ['convb', 'onesb'], [rb[c // 4]])
            kb.act(xcT[:].rearrange("p c t -> p (c t)"), ps[:, b:b + 3, :].rearrange("p b n -> p (b n)"), AF.Silu, rb, ['xcT'])
            bg, rbg = kb.banks(1)
            for c in range(12):
                kb.mm(ps[:, bg, 0:8], xcT[:, c, :], Gqk[:, c, :], c == 0, False, ['xcT', 'Gqk'], rbg)
            for c in range(12):
                kb.mm(ps[:, bg, 0:8], ut[:, c, 3:131], Gv[:, c, :], False, False, [ru, 'Gv'], rbg)
            kb.mm(ps[:, bg, 0:8], onesb[0:1, :], bgate[0:1, :], False, True, ['onesb', 'bgate'], rbg)
            g = gs[p]
            rg = P('gs')
            kb.cp('dve', g[:, 0:4], ps[:, bg, 0:4], rbg, [rg])
            kb.act(g[:, 4:8], ps[:, bg, 4:8], AF.Exp, rbg, [rg], scale=-1.0)
            kb.act(g[:, 8:12], g[:, 4:8], AF.Ln, [rg, 'epsc'], [rg], bias=epsc[:, 3:4])
            kb.ts('dve', g[:, 8:12], g[:, 8:12], -1.0, None, ALU.mult, None, [rg], [rg])
            b2, rb2 = kb.banks(1)
            kb.mm(ps[:, b2, 0:4], tri_incl[:], g[:, 8:12], True, True, ['tri_incl', rg], rb2)
            kb.mm(ps[:, b2, 4:8], onesf[:], g[:, 8:12], True, True, ['onesf', rg], rb2)
            kb.cp('dve', g[:, 12:20], ps[:, b2, 0:8], rb2, [rg])
            kb.tt('dve', g[:, 32:36], g[:, 0:4], g[:, 12:16], ALU.subtract, [rg], [rg])
            kb.act(g[:, 20:24], g[:, 32:36], AF.Exp, [rg], [rg])
            kb.act(g[:, 24:32], g[:, 12:20], AF.Exp, [rg], [rg])
            b, rb = kb.banks(3)
            for c in range(12):
                kb.mm(ps[:, b + c // 4, (c % 4) * 128:(c % 4 + 1) * 128], BD['bdq'][:, c, :], xcT[:, c, :], True, True, ['xcT', 'bdq'], [rb[c // 4]])
            kb.cp('act', qT[:].rearrange("p c t -> p (c t)"), ps[:, b:b + 3, :].rearrange("p b n -> p (b n)"), rb, ['qT'])
            b, rb = kb.banks(3)
            for c in range(12):
                kb.mm(ps[:, b + c // 4, (c % 4) * 128:(c % 4 + 1) * 128], BD['bdk'][:, c, :], xcT[:, c, :], True, True, ['xcT', 'bdk'], [rb[c // 4]])
            kb.act(kT[:].rearrange("p c t -> p (c t)"), ps[:, b:b + 3, :].rearrange("p b n -> p (b n)"), AF.Copy, rb, ['kT'], scale=384.0 ** -0.5)
            b, rb = kb.banks(3)
            for c in range(12):
                kb.mm(ps[:, b + c // 4, (c % 4) * 128:(c % 4 + 1) * 128], xcT[:, c, :], BD['bdk'][:, c, :], True, True, ['xcT', 'bdk'], [rb[c // 4]])
            for h in range(4):
                kb.ts('dve', kw[:, h, :], ps[:, b:b + 3, :].rearrange("p b n -> p (b n)")[:, h * 384:(h + 1) * 384], g[:, 20 + h:21 + h], 384.0 ** -0.5,
                      ALU.mult, ALU.mult, rb + [rg], ['kw'])
            b, rb = kb.banks(3)
            for c in range(12):
                kb.mm(ps[:, b + c // 4, (c % 4) * 128:(c % 4 + 1) * 128], ut[:, c, 3:131], BD['bdv'][:, c, :], True, True, [ru, 'bdv'], [rb[c // 4]])
            kb.cp('act', vx[:, :, 0:384], ps[:, b:b + 3, :].rearrange("p b n -> p (b n)").rearrange("p (h e) -> p h e", h=4), rb, ['vx'])
            kb.dma('pool', vfirst_d[ti * 128:(ti + 1) * 128, :].rearrange("t (h e) -> t h e", h=4), vx[:, :, 0:384], ['vx'], [('vfirst', ti)])
            b, rb = kb.banks(1)
            for h in range(4):
                for dc in range(3):
                    kb.mm(ps[:, b, h * 128:(h + 1) * 128], kT[:, h * 3 + dc, :], qT[:, h * 3 + dc, :], dc == 0, dc == 2, ['kT', 'qT'], rb)
            for h in range(4):
                kb.stt(PT[:, h, :], ps[:, b, h * 128:(h + 1) * 128], g[:, 20 + h:21 + h], tri_incl[:], ALU.mult, ALU.mult, rb + [rg, 'tri_incl'], ['PT'])
            bn, rbn = kb.banks(4)
            for h in range(4):
                kb.mm(ps[:, bn + h, 0:385], PT[:, h, :], vx[:, h, :], True, False, ['PT', 'vx', 'vx1'], [rbn[h]])
                for dc in range(3):
                    kb.mm(ps[:, bn + h, 0:385], qT[:, h * 3 + dc, :], Cb[:, h * 3 + dc, :], False, dc == 2, ['qT', 'Cb'], [rbn[h]])
            kb.tt('dve', g[:, 36:40], ps[:, bn:bn + 4, 384], g[:, 24:28], ALU.mult, rbn + [rg], [rg])
            kb.act(g[:, 36:40], g[:, 36:40], AF.Abs, [rg], [rg])
            kb.ts('dve', g[:, 36:40], g[:, 36:40], 1.0, None, ALU.max, None, [rg], [rg])
            kb.op('dve', lambda e: e.reciprocal(out=g[:, 40:44], in_=g[:, 36:40]), [rg], [rg])
            kb.tt('dve', g[:, 44:48], g[:, 40:44], g[:, 24:28], ALU.mult, [rg], [rg])
            for h in range(4):
                kb.act(hh[:, h * 384:(h + 1) * 384], ps[:, bn + h, 0:384], AF.Copy, [rbn[h], rg], ['hh'], scale=g[:, 44 + h:45 + h])
            for h in range(4):
                bs, rbs = kb.banks(3)
                for dc in range(3):
                    kb.mm(ps[:, bs + dc, 0:385], kw[:, h, dc * 128:(dc + 1) * 128], vx[:, h, :], True, True, ['kw', 'vx', 'vx1'], [rbs[dc]])
                kb.tt('dve', Cf[:, h * 3:h * 3 + 3, :], ps[:, bs:bs + 3, 0:385], Cf[:, h * 3:h * 3 + 3, :], ALU.add, rbs + ['Cf'], ['Cf'])
                kb.ts('dve', Cf[:, h * 3:h * 3 + 3, :], Cf[:, h * 3:h * 3 + 3, :], g[:, 28 + h:29 + h], None, ALU.mult, None, ['Cf', rg], ['Cf'])
                kb.cp('act', Cb[:, h * 3:h * 3 + 3, :], Cf[:, h * 3:h * 3 + 3, :], ['Cf'], ['Cb'])
            for h in range(4):
                kb.op('dve', lambda e: e.bn_stats(out=bst[:, h, :], in_=hh[:, h * 384:(h + 1) * 384]), ['hh'], ['bst'])
            for h in range(4):
                kb.op('dve', lambda e: e.bn_aggr(out=g[:, 48 + 2 * h:50 + 2 * h], in_=bst[:, h, :]), ['bst'], [rg])
            for h in range(4):
                kb.act(g[:, 56 + h:57 + h], g[:, 49 + 2 * h:50 + 2 * h], AF.Sqrt, [rg, 'epsc'], [rg], bias=epsc[:, 1:2])
            kb.op('dve', lambda e: e.reciprocal(out=g[:, 60:64], in_=g[:, 56:60]), [rg], [rg])
            for h in range(4):
                kb.ts('dve', hh[:, h * 384:(h + 1) * 384], hh[:, h * 384:(h + 1) * 384], g[:, 48 + 2 * h:49 + 2 * h], g[:, 60 + h:61 + h],
                      ALU.subtract, ALU.mult, ['hh', rg], ['hh'])
            kb.tt('dve', hh[:], hh[:], mhn_rep[:], ALU.mult, ['hh', 'mhn_rep'], ['hh'])
            b, rb = kb.banks(3)
            for c in range(12):
                kb.mm(ps[:, b + c // 4, (c % 4) * 128:(c % 4 + 1) * 128], xcT[:, c, :], diagS[:, c, :], True, True, ['xcT', 'diagS'], [rb[c // 4]])
            kb.tt('dve', hh[:], ps[:, b:b + 3, :].rearrange("p b n -> p (b n)"), hh[:], ALU.add, rb + ['hh'], ['hh'])
            kb.tt('dve', szb[p][:, 0:DMIX], hh[:], szb[p][:, 0:DMIX], ALU.mult, ['hh', P('szt')], [P('szt')])
            attn_and_out(ti, 0, szb[p], P('szt'), xtb[p], P('xt'), (sc, pbuf, pT, yT), xres_d if stage > 1 else out_d, final=False)
        kb.barrier()


    if stage < 2:
        kb.barrier()
        return kb

    CDEC = -0.6065306597126334
    with ExitStack() as st:
        mem_kv(1, st)
        kb.barrier()
    with ExitStack() as st:
        hT = kb.sb('hT1', [128, 8, S + 1], BF16, st)
        phase_A(xres_d, NT, norm_g_d[1:2, :], hT, 1, st, 'A1')
        hres_all = [('hT', i) for i in range(NT)] + ['hT_z']
        for c4 in range(4):
            kb.dma('pool', wout[:, c4 * 4:(c4 + 1) * 4, :], wout_d[1][c4 * 512:(c4 + 1) * 512, :].rearrange("(c p) n -> p c n", p=128), (), ['wout'])
        wst = kb.sb('wst', [128, 8, 512], F32, st)
        mur = kb.sb('mur', [128, 512], F32, st)
        omu = kb.sb('omu', [128, 512], F32, st)
        wbs = [kb.sb('r_wb%d' % i, [128, 8, 512], BF16, st) for i in range(2)]
        wbs2 = [kb.sb('r_wc%d' % i, [128, 8, 512], BF16, st) for i in range(2)]
        stg = [kb.sb('r_stg%d' % i, [128, 512], BF16, st) for i in range(3)]
        nstg = 0

        def load_shifted(c0, ncols, par):
            kb.dma('sp', wst[:, :, 0:ncols], rww_d[:, c0:c0 + ncols].rearrange("(k p) c -> p k c", p=128), (), ['wst'])
            kb.dma('sp', mur[:, 0:ncols], mu_d[0:1, c0:c0 + ncols].to_broadcast([128, ncols]), (), ['mur'])
            kb.ts('dve', omu[:, 0:ncols], mur[:, 0:ncols], -1.0, 1.0, ALU.mult, ALU.add, ['mur'], ['omu'])
            kb.tt('dve', wbs2[par][:, :, 0:ncols], wst[:, :, 0:ncols], mur[:, 0:ncols].unsqueeze(1).to_broadcast([128, 8, ncols]), ALU.mult,
                  ['wst', 'mur'], ['wc%d' % par])
            kb.tt('dve', wbs[par][:, :, 0:ncols], wst[:, :, 0:ncols], omu[:, 0:ncols].unsqueeze(1).to_broadcast([128, 8, ncols]), ALU.mult,
                  ['wst', 'omu'], ['wb%d' % par])

        nblk = 0
        for blk in range(9):
            par = nblk % 2
            nblk += 1
            load_shifted(blk * 512, 512, par)
            for ti in range(NT):
                b, rb = proj_T(hT, 1, ti, [('hT', ti), ('hT', max(ti - 1, 0)), 'hT_z'], wbs[par], 'wb%d' % par, 512, wbs2[par], 'wc%d' % par)
                sg_ = stg[nstg % 3]
                rs = 'stg%d' % (nstg % 3)
                nstg += 1
                kb.cp('act' if ti % 2 else 'dve', sg_[:], ps[:, b, :], rb, [rs])
                kb.dma('pool', rkv_d[ti * 128:(ti + 1) * 128, blk * 512:(blk + 1) * 512], sg_[:], [rs], [('rkv', ti)])
        par = nblk % 2
        nblk += 1
        load_shifted(4608, 288, par)
        for tb in range(4):
            cs = slice(3 + tb * 512, 3 + (tb + 1) * 512)
            fw = [('featT', tb * 4 + q) for q in range(4)]
            b, rb = proj_F(hT, 1, tb * 512, 512, hres_all, wbs[par], 'wb%d' % par, 0, 128, wbs2[par], 'wc%d' % par)
            kb.act(featT[0:64, 4, cs], ps[0:64, b, :], AF.Tanh, rb, fw)
            kb.cp('dve', featT[64:128, 4, cs], ps[64:128, b, :], rb, fw)
            b, rb = proj_F(hT, 1, tb * 512, 512, hres_all, wbs[par], 'wb%d' % par, 128, 32, wbs2[par], 'wc%d' % par)
            kb.cp('dve', featT[0:32, 5, cs], ps[0:32, b, :], rb, fw)
            b, rb = proj_F(hT, 1, tb * 512, 512, hres_all, wbs[par], 'wb%d' % par, 160, 128, wbs2[par], 'wc%d' % par)
            kb.act(featT[:, 6, cs], ps[:, b, :], AF.Sigmoid, rb, fw)
        par = nblk % 2
        nblk += 1
        load_w_block(rww_d, 4896, 512, wbs[par], 'wb%d' % par)
        for m in range(4):
            for tb in range(4):
                b, rb = proj_F(hT, 1, tb * 512, 512, hres_all, wbs[par], 'wb%d' % par, m * 128, 128)
                kb.cp('act' if tb % 2 else 'dve', featT[:, m, 3 + tb * 512: 3 + (tb + 1) * 512], ps[:, b, :], rb,
                      [('featT', tb * 4 + q) for q in range(4)])
        for zb in range(4):
            par = nblk % 2
            nblk += 1
            load_w_block(rww_d, 5408 + zb * 512, 512, wbs[par], 'wb%d' % par)
            for ti in range(NT):
                b, rb = proj_T(hT, 1, ti, [('hT', ti)], wbs[par], 'wb%d' % par, 512)
                sg_ = stg[nstg % 3]
                rs = 'stg%d' % (nstg % 3)
                nstg += 1
                kb.act(sg_[:], ps[:, b, :], AF.Silu, rb, [rs])
                kb.dma('pool', sz_d[ti * 128:(ti + 1) * 128, zb * 512:(zb + 1) * 512], sg_[:], [rs], [('sz', ti)])
        kb.barrier()

    if stage == 3:
        kb.barrier()
        return kb

    with ExitStack() as st:
        wa2 = kb.sb('wa2', [128, DMIX], BF16, st)
        vl2 = kb.sb('vl2', [32, DMIX], BF16, st)
        gl2 = kb.sb('gl2', [128, DMIX], BF16, st)
        brows = kb.sb('brows', [3, DMIX], BF16, st)
        sel = kb.sb('sel', [3, 3, 128], BF16, st)
        reps = kb.sb('reps', [128, 5, DMIX], BF16, st)
        tsc_incl = kb.sb('tsc_incl', [128, 128], F32, st)
        tsc_strict = kb.sb('tsc_strict', [128, 128], F32, st)
        tsc_rev = kb.sb('tsc_rev', [128, 128], F32, st)
        mST2 = kb.sb('mST2', [128, 4, 128], BF16, st)
        mTS = kb.sb('mTS', [128, 4, 128], BF16, st)
        negcol = kb.sb('negcol', [128, 2], F32, st)
        STf = kb.sb('STf', [64, 24, 64], F32, st)
        STb = kb.sb('STb', [64, 24, 64], BF16, st)
        kb.dma('pool', wa2[:], wa2_d, (), ['wa2'])
        kb.dma('pool', vl2[:], vl2_d, (), ['vl2'])
        kb.dma('pool', gl2[:], gl2_d, (), ['gl2'])
        kb.dma('pool', brows[:], brow_d, (), ['brows'])
        for i in range(5):
            kb.dma('pool', reps[:, i, :], reps_d[i:i + 1, :].to_broadcast([128, DMIX]), (), ['reps'])
        kb.dma('sp', grep[:], fing_d.to_broadcast([128, D]), (), ['grep'])
        for r_ in range(3):
            kb.cp('dve', sel[:, r_, :], identb[0:3, r_:r_ + 1].to_broadcast([3, 128]), ['identb'], ['sel'])
        kb.ts('dve', tsc_incl[:], tri_incl[:], CDEC, None, ALU.mult, None, ['tri_incl'], ['tsc_incl'])
        kb.op('dve', lambda e: e.tensor_single_scalar(out=tsc_strict[:], in_=iot[:], scalar=0, op=ALU.is_gt), ['iot'], ['tsc_strict'])
        kb.ts('dve', tsc_strict[:], tsc_strict[:], CDEC, None, ALU.mult, None, ['tsc_strict'], ['tsc_strict'])
        kb.op('dve', lambda e: e.tensor_single_scalar(out=tsc_rev[:], in_=iot[:], scalar=0, op=ALU.is_lt), ['iot'], ['tsc_rev'])
        kb.ts('dve', tsc_rev[:], tsc_rev[:], CDEC, None, ALU.mult, None, ['tsc_rev'], ['tsc_rev'])
        for q in range(4):
            kb.op('dve', lambda e: e.tensor_single_scalar(out=mST2[:, q, :], in_=iot[:], scalar=0, op=(ALU.is_gt if q % 2 == 0 else ALU.is_ge)), ['iot'], ['mST2'])
            kb.op('dve', lambda e: e.tensor_single_scalar(out=mTS[:, q, :], in_=iot[:], scalar=0, op=ALU.is_lt), ['iot'], ['mTS'])
        kb.memset('dve', negcol[:], CDEC, ['negcol'])
        kb.memset('dve', STf[:], 0.0, ['STf'])
        kb.memset('dve', STb[:], 0.0, ['STb'])

        NB = 1
        szb = [kb.sb('r_szt%d' % i, [128, 2048], BF16, st) for i in range(NB)]
        xtb = [kb.sb('r_xt%d' % i, [128, D], F32, st) for i in range(NB)]
        rkv = kb.sb('r_rkv', [128, 4608], BF16, st)
        sg = kb.sb('r_sg', [128, DMIX], F32, st)
        av = kb.sb('r_av', [128, DMIX], BF16, st)
        gg = kb.sb('r_gg', [128, DMIX], BF16, st)
        V = kb.sb('r_V', [128, DMIX], BF16, st)
        kkn = kb.sb('r_kkn', [128, DMIX], BF16, st)
        kp = kb.sb('r_kp', [128, DMIX], BF16, st)
        bv = kb.sb('r_bv', [128, DMIX], BF16, st)
        bon = kb.sb('r_bon', [128, DMIX], BF16, st)
        ef = kb.sb('r_ef', [128, DMIX], F32, st)
        tmpf = kb.sb('r_tmpf', [128, DMIX], F32, st)
        at = kb.sb('r_at', [128, DMIX], BF16, st)
        xs = kb.sb('r_xs', [128, DMIX], BF16, st)
        vf = xs
        bbar = kkn
        kbar = av
        XT = kb.sb('r_XT', [128, 12, 4, 128], BF16, st)
        sm = kb.sb('r_sm', [128, 128], F32, st)
        WLT = kb.sb('r_WLT', [64, 24], F32, st)
        rt = kb.sb('r_rt', [128, DMIX], BF16, st)
        ATab = kb.sb('r_ATab', [128, 4, 2, 128], BF16, st)
        ATak = kb.sb('r_ATak', [128, 4, 2, 128], BF16, st)
        Qb = [kb.sb('r_Q%d' % i, [128, 4, 128], BF16, st) for i in range(2)]
        Pb = [kb.sb('r_P%d' % i, [128, 4, 128], BF16, st) for i in range(2)]
        TTb = [kb.sb('r_TT%d' % i, [128, 4, 128], BF16, st) for i in range(2)]
        X1s = kb.sb('r_X1s', [128, 4, 64], BF16, st)
        AU = kb.sb('r_AU', [128, 4, 128], BF16, st)
        RT = kb.sb('r_RT', [64, 4, 128], BF16, st)
        MT = kb.sb('r_MT', [64, 4, 64], BF16, st)
        sc = kb.sb('r_sc', [128, 24], F32, st)
        pbuf = kb.sb('r_pbuf', [128, 4, NMEM], BF16, st)
        pT = kb.sb('r_pT', [128, 1024], BF16, st)
        yT = XT[:, 0:4, :, :].rearrange("p a b t -> p (a b) t")
        kb.barrier()

        def flat3(b):
            return ps[:, b:b + 3, :].rearrange("p b n -> p (b n)")

        def hv(t):
            return t.rearrange("p (h n) -> p h n", n=64)

        def bc(t24):
            return t24.unsqueeze(2).to_broadcast([128, 24, 64])

        for ti in range(NT if stage == 2 else 1):
            p = ti % NB
            P = lambda n: '%s%d' % (n, p)
            rows = slice(ti * 128, (ti + 1) * 128)
            tok = slice(3 + ti * 128, 3 + (ti + 1) * 128)
            fr = [('featT', ti)]
            kb.dma('sp', rkv[:], rkv_d[rows, :], [('rkv', ti)], ['rkv'])
            kb.dma('sp', vf[:], vfirst_d[rows, :], [('vfirst', ti)], ['xs'])
            kb.dma('sp', szb[p][:], sz_d[rows, :], [('sz', ti)], [P('szt')])
            kb.dma('sp', xtb[p][:], xres_d[rows, :], [('xres', ti)], [P('xt')])
            r_ = rkv[:, 0:DMIX]
            k_ = rkv[:, DMIX:2 * DMIX]
            v_ = rkv[:, 2 * DMIX:3 * DMIX]
            b, rb = kb.banks(3)
            for nb in range(3):
                cs = slice(nb * 512, (nb + 1) * 512)
                kb.mm(ps[:, b + nb, :], featT[0:64, 4, tok], wa2[0:64, cs], True, False, fr + ['wa2'], [rb[nb]])
                kb.mm(ps[:, b + nb, :], sel[0:3, 0, :], brows[0:3, cs], False, True, ['sel', 'brows'], [rb[nb]])
            kb.act(sg[:], flat3(b), AF.Sigmoid, rb, ['sg'])
            b, rb = kb.banks(3)
            for nb in range(3):
                cs = slice(nb * 512, (nb + 1) * 512)
                kb.mm(ps[:, b + nb, :], featT[64:128, 4, tok], wa2[64:128, cs], True, False, fr + ['wa2'], [rb[nb]])
                kb.mm(ps[:, b + nb, :], sel[0:3, 1, :], brows[0:3, cs], False, True, ['sel', 'brows'], [rb[nb]], ser=True)
            kb.act(av[:], flat3(b), AF.Sigmoid, rb, ['av'])
            b, rb = kb.banks(3)
            for nb in range(3):
                cs = slice(nb * 512, (nb + 1) * 512)
                kb.mm(ps[:, b + nb, :], featT[0:32, 5, tok], vl2[0:32, cs], True, False, fr + ['vl2'], [rb[nb]])
                kb.mm(ps[:, b + nb, :], sel[0:3, 2, :], brows[0:3, cs], False, True, ['sel', 'brows'], [rb[nb]])
            kb.act(bon[:], flat3(b), AF.Sigmoid, rb, ['bon'])
            b, rb = kb.banks(3)
            for nb in range(3):
                cs = slice(nb * 512, (nb + 1) * 512)
                kb.mm(ps[:, b + nb, :], featT[:, 6, tok], gl2[:, cs], True, True, fr + ['gl2'], [rb[nb]])
            kb.cp('act', gg[:], flat3(b), rb, ['gg'])
            kb.tt('dve', tmpf[:], vf[:], v_, ALU.subtract, ['xs', 'rkv'], ['tmpf'])
            kb.tt('dve', tmpf[:], tmpf[:], bon[:], ALU.mult, ['tmpf', 'bon'], ['tmpf'])
            kb.tt('dve', V[:], tmpf[:], v_, ALU.add, ['tmpf', 'rkv'], ['V'])
            kb.tt('dve', kkn[:], k_, reps[:, 0, :], ALU.mult, ['rkv', 'reps'], ['kkn'])
            kb.tt('dve', tmpf[:], kkn[:], kkn[:], ALU.mult, ['kkn', 'tmpf'], ['tmpf'])
            kb.op('dve', lambda e: e.tensor_reduce(out=sm[:, 0:24], in_=hv(tmpf[:]), axis=AX.X, op=ALU.add), ['tmpf'], ['sm'])
            kb.act(sm[:, 0:24], sm[:, 0:24], AF.Sqrt, ['sm'], ['sm'])
            kb.ts('dve', sm[:, 0:24], sm[:, 0:24], 1e-12, None, ALU.max, None, ['sm'], ['sm'])
            kb.op('dve', lambda e: e.reciprocal(out=sm[:, 24:48], in_=sm[:, 0:24]), ['sm'], ['sm'])
            kb.tt('dve', hv(kkn[:]), hv(kkn[:]), bc(sm[:, 24:48]), ALU.mult, ['kkn', 'sm'], ['kkn'])
            kb.stt(tmpf[:], av[:], -1.0, reps[:, 1, :], ALU.add, ALU.mult, ['av', 'reps', 'tmpf'], ['tmpf'])
            kb.stt(kp[:], tmpf[:], 1.0, k_, ALU.add, ALU.mult, ['tmpf', 'rkv'], ['kp'])
            kb.tt('dve', bv[:], kkn[:], av[:], ALU.mult, ['kkn', 'av'], ['bv'])
            kb.tt('dve', tmpf[:], r_, kp[:], ALU.mult, ['rkv', 'kp', 'tmpf'], ['tmpf'])
            kb.tt('dve', tmpf[:], tmpf[:], reps[:, 2, :], ALU.mult, ['tmpf', 'reps'], ['tmpf'])
            kb.op('dve', lambda e: e.tensor_reduce(out=sm[:, 48:72], in_=hv(tmpf[:]), axis=AX.X, op=ALU.add), ['tmpf'], ['sm2'])
            kb.tt('dve', hv(bon[:]), hv(V[:]), bc(sm[:, 48:72]), ALU.mult, ['V', 'sm2', 'bon'], ['bon'])
            bw, rbw = kb.banks(1)
            for h_ in range(24):
                kb.mm(ps[0:64, bw, h_ * 2:h_ * 2 + 2], sg[:, h_ * 64:(h_ + 1) * 64], negcol[:, 0:2], True, True, ['sg', 'negcol'], rbw)
            kb.act(WLT[:], ps[0:64, bw, 0:48].rearrange("p (h two) -> p h two", two=2)[:, :, 0], AF.Exp, rbw, ['WLT'])
            b, rb = kb.banks(3)
            for nb in range(3):
                kb.mm(ps[:, b + nb, :], tsc_incl[:], sg[:, nb * 512:(nb + 1) * 512], True, True, ['tsc_incl', 'sg'], [rb[nb]])
            kb.act(ef[:], flat3(b), AF.Exp, rb, ['ef'])

            def transposes(idx):
                b_, rb_ = kb.banks(2)
                for hp in range(12):
                    kb.tr(psb[:, b_ + hp // 8, (hp % 8) * 128:(hp % 8 + 1) * 128], xs[:, hp * 128:(hp + 1) * 128], identb[:], ['xs', 'identb'], [rb_[hp // 8]])
                kb.cp('act', XT[:, 0:8, idx, :], psb[:, b_, :].rearrange("p (c t) -> p c t", c=8), [rb_[0]], ['XT', 'yT0', 'yT1'])
                kb.cp('act', XT[:, 8:12, idx, :], psb[:, b_ + 1, 0:512].rearrange("p (c t) -> p c t", c=4), [rb_[1]], ['XT', 'yT0', 'yT1'])

            kb.tt('dve', xs[:], r_, ef[:], ALU.mult, ['rkv', 'ef'], ['xs'])
            transposes(1)
            kb.cp('dve', rt[:], xs[:], ['xs'], ['rt'])
            kb.op('dve', lambda e: e.reciprocal(out=ef[:], in_=ef[:]), ['ef'], ['ef'])
            kb.tt('dve', xs[:], bv[:], ef[:], ALU.mult, ['bv', 'ef'], ['xs'])
            transposes(2)
            kb.tt('dve', xs[:], kp[:], ef[:], ALU.mult, ['kp', 'ef'], ['xs'])
            transposes(3)
            b, rb = kb.banks(3)
            for nb in range(3):
                kb.mm(ps[:, b + nb, :], tsc_strict[:], sg[:, nb * 512:(nb + 1) * 512], True, True, ['tsc_strict', 'sg'], [rb[nb]])
            kb.act(ef[:], flat3(b), AF.Exp, rb, ['ef'])
            kb.stt(at[:], kkn[:], -1.0, ef[:], ALU.mult, ALU.mult, ['kkn', 'ef'], ['at'])
            kb.cp('dve', xs[:], at[:], ['at'], ['xs'])
            transposes(0)
            b, rb = kb.banks(3)
            for nb in range(3):
                kb.mm(ps[:, b + nb, :], tsc_rev[:], sg[:, nb * 512:(nb + 1) * 512], True, True, ['tsc_rev', 'sg'], [rb[nb]])
            kb.act(ef[:], flat3(b), AF.Exp, rb, ['ef'])
            kb.tt('dve', bbar[:], bv[:], ef[:], ALU.mult, ['bv', 'ef'], ['kkn'])
            kb.tt('dve', kbar[:], kp[:], ef[:], ALU.mult, ['kp', 'ef'], ['av'])

            for gI in range(6 if stage in (2, 4) else (0 if stage == 5 else 1)):
                hidx = lambda q: 4 * gI + 2 * (q % 2) + q // 2
                hcol = lambda q: slice(hidx(q) * 64, (hidx(q) + 1) * 64)
                bA, rbA = kb.banks(4)
                for q in range(4):
                    hp = 2 * gI + q % 2
                    prt = slice(64 * (q // 2), 64 * (q // 2) + 64)
                    rhsAR = XT[prt, hp, 0:2, :].rearrange("p a t -> p (a t)")
                    co = slice((q % 2) * 256, (q % 2) * 256 + 256)
                    kb.mm(ps[:, bA + (q // 2) * 2, co], XT[prt, hp, 2, :], rhsAR, True, True, ['XT'], [rbA[(q // 2) * 2]])
                    kb.mm(ps[:, bA + (q // 2) * 2 + 1, co], XT[prt, hp, 3, :], rhsAR, True, True, ['XT'], [rbA[(q // 2) * 2 + 1]])
                for pr in range(2):
                    kb.tt('dve', ATab[:, 2 * pr:2 * pr + 2, :, :].rearrange("p q k t -> p (q k t)"), ps[:, bA + 2 * pr, :], mST2[:].rearrange("p q t -> p (q t)"),
                          ALU.mult, [rbA[2 * pr], 'mST2'], ['ATab'])
                    kb.tt('dve', ATak[:, 2 * pr:2 * pr + 2, :, :].rearrange("p q k t -> p (q k t)"), ps[:, bA + 2 * pr + 1, :], mST2[:].rearrange("p q t -> p (q t)"),
                          ALU.mult, [rbA[2 * pr + 1], 'mST2'], ['ATak'])
                bZ, rbZ = kb.banks(2)
                for q in range(4):
                    hp = 2 * gI + q % 2
                    prt = slice(64 * (q // 2), 64 * (q // 2) + 64)
                    kb.mm(ps[:, bZ + q // 2, (q % 2) * 128:(q % 2 + 1) * 128], XT[prt, hp, 0, :], XT[prt, hp, 2, :], True, True, ['XT'], [rbZ[q // 2]])
                for hd in range(2):
                    kb.tt('dve', Pb[0][:, 2 * hd:2 * hd + 2, :].rearrange("p q t -> p (q t)"), ps[:, bZ + hd, 0:256], mTS[:, 0:2, :].rearrange("p q t -> p (q t)"),
                          ALU.mult, [rbZ[hd], 'mTS'], ['P0'])
                kb.tt('dve', TTb[0][:], ATab[:, :, 0, :], identb[:].unsqueeze(1).to_broadcast([128, 4, 128]), ALU.add, ['ATab', 'identb'], ['TT0'])
                cur = 0
                for lev in range(1, 8):
                    Qc = [ATab[:, q, 0, :] for q in range(4)] if lev == 1 else [Qb[cur][:, q, :] for q in range(4)]
                    rQ = 'ATab' if lev == 1 else 'Q%d' % cur
                    Pc = [Pb[cur][:, q, :] for q in range(4)]
                    rP = 'P%d' % cur
                    nxt = 1 - cur
                    if lev >= 2:
                        bX, rbX = kb.banks(1)
                        for q in range(4):
                            kb.mm(ps[:, bX, q * 128:(q + 1) * 128], Pc[q], TTb[cur][:, q, :], True, True, [rP, 'TT%d' % cur], rbX)
                    if lev <= 6:
                        bP, rbP = kb.banks(1)
                        for q in range(4):
                            kb.mm(ps[:, bP, q * 128:(q + 1) * 128], Qc[q], Pc[q], True, True, [rQ, rP], rbP)
                    if lev <= 5:
                        bQ, rbQ = kb.banks(1)
                        for q in range(4):
                            kb.mm(ps[:, bQ, q * 128:(q + 1) * 128], Pc[q], Qc[q], True, True, [rQ, rP], rbQ)
                    if lev >= 2:
                        kb.tt('dve', TTb[nxt][:].rearrange("p q t -> p (q t)"), ps[:, bX, :], TTb[cur][:].rearrange("p q t -> p (q t)"), ALU.add,
                              rbX + ['TT%d' % cur], ['TT%d' % nxt])
                    else:
                        kb.cp('dve', TTb[nxt][:], TTb[cur][:], ['TT%d' % cur], ['TT%d' % nxt])
                    if lev <= 6:
                        kb.cp('act', Pb[nxt][:].rearrange("p q t -> p (q t)"), ps[:, bP, :], rbP, ['P%d' % nxt])
                    if lev <= 5:
                        kb.cp('act', Qb[nxt][:].rearrange("p q t -> p (q t)"), ps[:, bQ, :], rbQ, ['Q%d' % nxt])
                    cur = nxt
                TT = TTb[cur]
                rTT = 'TT%d' % cur
                bX, rbX = kb.banks(1)
                for q in range(4):
                    kb.mm(ps[:, bX, q * 64:(q + 1) * 64], ATak[:, q, 0, :], V[:, hcol(q)], True, True, ['ATak', 'V'], rbX)
                kb.cp('act', X1s[:].rearrange("p q e -> p (q e)"), ps[:, bX, 0:256], rbX, ['X1s'])
                bU, rbU = kb.banks(1)
                for q in range(4):
                    kb.mm(ps[:, bU, q * 128:q * 128 + 64], TT[:, q, :], at[:, hcol(q)], True, True, [rTT, 'at'], rbU)
                    kb.mm(ps[:, bU, q * 128 + 64:(q + 1) * 128], TT[:, q, :], X1s[:, q, :], True, True, [rTT, 'X1s'], rbU)
                kb.cp('act', AU[:].rearrange("p q e -> p (q e)"), ps[:, bU, :], rbU, ['AU'])
                bR, rbR = kb.banks(1)
                for q in range(4):
                    kb.mm(ps[0:64, bR, q * 128:(q + 1) * 128], AU[:, q, 0:64], ATab[:, q, 1, :], True, False, ['AU', 'ATab'], rbR)
                    kb.mm(ps[0:64, bR, q * 128:(q + 1) * 128], rt[:, hcol(q)], identb[:], False, True, ['rt', 'identb'], rbR)
                kb.cp('act', RT[:].rearrange("p q t -> p (q t)"), ps[0:64, bR, :], rbR, ['RT'])
                bM, rbM = kb.banks(1)
                for q in range(4):
                    kb.mm(ps[0:64, bM, q * 64:(q + 1) * 64], AU[:, q, 0:64], bbar[:, hcol(q)], True, True, ['AU', 'kkn'], rbM)
                kb.cp('act', MT[:].rearrange("p a e -> p (a e)"), ps[0:64, bM, 0:256], rbM, ['MT'])
                bY, rbY = kb.banks(1)
                for q in range(4):
                    o = ps[:, bY, q * 64:(q + 1) * 64]
                    kb.mm(o, ATab[:, q, 1, :], AU[:, q, 64:128], True, False, ['ATab', 'AU'], rbY)
                    kb.mm(o, ATak[:, q, 1, :], V[:, hcol(q)], False, False, ['ATak', 'V'], rbY)
                    kb.mm(o, RT[:, q, :], STb[:, hidx(q), :], False, True, ['RT', 'STb'], rbY)
                for q in range(4):
                    kb.cp('act', tmpf[:, hcol(q)], ps[:, bY, q * 64:(q + 1) * 64], rbY, ['yo'])
                bS, rbS = kb.banks(1)
                for q in range(4):
                    o = ps[0:64, bS, q * 64:(q + 1) * 64]
                    kb.mm(o, MT[:, q, :], STb[:, hidx(q), :], True, False, ['MT', 'STb'], rbS)
                    kb.mm(o, bbar[:, hcol(q)], AU[:, q, 64:128], False, False, ['kkn', 'AU'], rbS)
                    kb.mm(o, kbar[:, hcol(q)], V[:, hcol(q)], False, True, ['av', 'V'], rbS)
                for q in range(4):
                    h_ = hidx(q)
                    kb.stt(STf[:, h_, :], STf[:, h_, :], WLT[:, h_:h_ + 1], ps[0:64, bS, q * 64:(q + 1) * 64], ALU.mult, ALU.add, rbS + ['STf', 'WLT'], ['STf'])
                kb.cp('act', STb[:, 4 * gI:4 * gI + 4, :], STf[:, 4 * gI:4 * gI + 4, :], ['STf'], ['STb'])

            yo = tmpf
            kb.op('dve', lambda e: e.tensor_reduce(out=sm[:, 72:96], in_=hv(yo[:]), axis=AX.X, op=ALU.add), ['yo', 'tmpf'], ['sm3'])
            kb.ts('dve', sm[:, 72:96], sm[:, 72:96], 1.0 / 64, None, ALU.mult, None, ['sm3'], ['sm3'])
            kb.tt('dve', hv(yo[:]), hv(yo[:]), bc(sm[:, 72:96]), ALU.subtract, ['yo', 'sm3'], ['yo'])
            kb.tt('dve', ef[:], yo[:], yo[:], ALU.mult, ['yo', 'ef'], ['ef'])
            kb.op('dve', lambda e: e.tensor_reduce(out=sm[:, 96:120], in_=hv(ef[:]), axis=AX.X, op=ALU.add), ['ef'], ['sm4'])
            kb.act(sm[:, 96:120], sm[:, 96:120], AF.Sqrt, ['sm4', 'epsc'], ['sm4'], scale=1.0 / 64, bias=epsc[:, 2:3])
            kb.op('dve', lambda e: e.reciprocal(out=sm[:, 96:120], in_=sm[:, 96:120]), ['sm4'], ['sm4'])
            kb.tt('dve', hv(yo[:]), hv(yo[:]), bc(sm[:, 96:120]), ALU.mult, ['yo', 'sm4'], ['yo'])
            kb.tt('dve', yo[:], yo[:], reps[:, 3, :], ALU.mult, ['yo', 'reps'], ['yo'])
            kb.tt('dve', yo[:], yo[:], reps[:, 4, :], ALU.add, ['yo', 'reps'], ['yo'])
            kb.tt('dve', yo[:], yo[:], bon[:], ALU.add, ['yo', 'bon'], ['yo'])
            kb.tt('dve', yo[:], yo[:], gg[:], ALU.mult, ['yo', 'gg'], ['yo'])
            kb.tt('dve', szb[p][:, 0:DMIX], yo[:], szb[p][:, 0:DMIX], ALU.mult, ['yo', P('szt')], [P('szt'), 'tmpf'])
            attn_and_out(ti, 0, szb[p], P('szt'), xtb[p], P('xt'), (sc, pbuf, pT, yT), out_d, final=True, yx=['XT'])
        kb.barrier()

    kb.barrier()
    return kb


def _bd(w):
    n = w.shape[0]
    bd = np.zeros((12, 128, 128), np.float32)
    bdT = np.zeros((12, 128, 128), np.float32)
    for c in range(12):
        for j in range(32):
            blk = w[c * 32 + j]
            bd[c, 4 * j:4 * j + 4, 4 * j:4 * j + 4] = blk
            bdT[c, 4 * j:4 * j + 4, 4 * j:4 * j + 4] = blk.T
    return (np.ascontiguousarray(bd.transpose(1, 0, 2)), np.ascontiguousarray(bdT.transpose(1, 0, 2)))


_NC_CACHE = {}


def kernel(stage=2, **inp):
    f = lambda a: np.ascontiguousarray(np.asarray(a, dtype=np.float32))
    if stage not in _NC_CACHE:
        _NC_CACHE[stage] = build(stage)
    kb = _NC_CACHE[stage]
    shared = {
        'norm_g': f(inp['norm_g']), 'mem_norm_g': f(inp['mem_norm_g']), 'final_g': f(inp['final_g']).reshape(1, D),
        'kv0': f(inp['mem_kv_w'][0]), 'kv1': f(inp['mem_kv_w'][1]),
        'wout0': f(inp['w_out'][0]), 'wout1': f(inp['w_out'][1]),
        'ml_w_in': f(inp['ml_w_in'][0]),
        'cwT': f(np.asarray(inp['ml_conv_w'][0]).T.reshape(12, 128, 4).transpose(1, 0, 2)),
        'conv_b': f(inp['ml_conv_b'][0]).reshape(1, DMIX),
        'wg': f(np.asarray(inp['ml_w_gate'][0]).reshape(36, 128, 8).transpose(1, 0, 2)),
        'b_gate': f(inp['ml_b_gate'][0]).reshape(1, 8),
        'mhn_g': f(inp['ml_mhn_g'][0]).reshape(1, DMIX), 'skipT': f(np.asarray(inp['ml_skip'][0]).reshape(12, 128).T),
    }
    shared.update({
        'rw_w_in': f(inp['rw_w_in'][0]), 'rw_mu': f(inp['rw_mu'][0]).reshape(1, 4896),
        'wa_lora2': f(np.concatenate([np.asarray(inp['rw_w_lora2'][0]), np.asarray(inp['rw_a_lora2'][0])], axis=0)),
        'v_lora2': f(inp['rw_v_lora2'][0]), 'g_lora2': f(inp['rw_g_lora2'][0]),
        'brows': f(np.stack([np.asarray(inp['rw_w0'][0]), np.asarray(inp['rw_a0'][0]), np.asarray(inp['rw_v0'][0])], axis=0)),
        'reps': f(np.stack([np.asarray(inp['rw_k_k'][0]), np.asarray(inp['rw_k_a'][0]), np.asarray(inp['rw_r_k'][0]).reshape(DMIX),
                            np.asarray(inp['rw_lnx_g'][0]), np.asarray(inp['rw_lnx_b'][0])], axis=0)),
    })
    for n, k in [('bdq', 'ml_wq'), ('bdk', 'ml_wk'), ('bdv', 'ml_wv')]:
        bd, bdT = _bd(np.asarray(inp[k][0], dtype=np.float32))
        shared[n] = bd
        shared[n + 'T'] = bdT
    x = np.asarray(inp['x'], dtype=np.float32)
    mem = np.asarray(inp['mem'], dtype=np.float32)
    in_maps = []
    for c in range(8):
        m = dict(shared)
        m['x'] = np.ascontiguousarray(x[c])
        m['mem'] = np.ascontiguousarray(mem[c])
        in_maps.append(m)
    res = run_bass_kernel_spmd(kb.nc, in_maps, core_ids=list(range(8)))
    return np.stack([np.asarray(r['out'], dtype=np.float32) for r in res.results], axis=0)
```
